# Optimizing a Trainium2 kernel written in Bass

```python
import math, functools
import jax, jax.numpy as jnp
from jax import lax
import numpy as np

D_MODEL = 1024
BATCH = 2
SEQ = 8192
DEPTH = 2
DEC_BATCH = 128
DEC_SEQ = 1
PAST_LEN = 2048
PAGE_SIZE = 128

MIX_WIDTH = D_MODEL
POOL_WIDTH = MIX_WIDTH // 2
POOL_WINDOWS = (2, 4, 8, 16)
N_POOL_GROUPS = len(POOL_WINDOWS)
POOL_GROUP = POOL_WIDTH // N_POOL_GROUPS
POOL_PAST = max(POOL_WINDOWS) - 1
ATTN_WIDTH = MIX_WIDTH - POOL_WIDTH
DIFF_HD = 64
N_DIFF_HEADS = ATTN_WIDTH // (2 * DIFF_HD)
QK_WIDTH = N_DIFF_HEADS * 2 * DIFF_HD
IN_WIDTH = POOL_WIDTH + 2 * QK_WIDTH + ATTN_WIDTH
N_MEM = 256
N_CROSS_HEADS = 4
CROSS_HD = D_MODEL // N_CROSS_HEADS
D_FF = 2816
CONV_W = 3
Q_BLOCK = 128
EPS = 1e-6
SUBLN_EPS = 1e-5

kernel_name = 'hymba_pool_diffattn_convffn_step'


def rmsnorm(x, g, eps=EPS):
    xf = x.astype(jnp.float32)
    y = xf * lax.rsqrt(jnp.mean(xf * xf, axis=-1, keepdims=True) + eps)
    return (y * g.astype(jnp.float32)).astype(x.dtype)


def alibi_slopes():
    return jnp.asarray(2.0 ** (-8.0 * np.arange(1, N_DIFF_HEADS + 1) / N_DIFF_HEADS), jnp.float32)


def lambda_init(layer_idx):
    return 0.8 - 0.6 * math.exp(-0.3 * layer_idx)


def pool_mix(u, prefix, start_pos, w_pool, scale):
    B, T, C = u.shape
    up = jnp.concatenate([prefix.astype(u.dtype), u], axis=1)
    upf = up.astype(jnp.float32)
    cs = jnp.concatenate([jnp.zeros((B, 1, C), jnp.float32), jnp.cumsum(upf, axis=1)], axis=1)
    end = cs[:, POOL_PAST + 1:]
    pos = start_pos + jnp.arange(T)
    uf = u.astype(jnp.float32)
    groups = []
    for g, w in enumerate(POOL_WINDOWS):
        sl = slice(g * POOL_GROUP, (g + 1) * POOL_GROUP)
        wsum = end[..., sl] - cs[:, POOL_PAST + 1 - w:POOL_PAST + 1 - w + T, sl]
        cnt = jnp.minimum(pos + 1, w).astype(jnp.float32)[None, :, None]
        groups.append(wsum / cnt - uf[..., sl])
    p = jnp.stack(groups, axis=2)
    y = jnp.einsum('btgc,gcd->btgd', p, w_pool.astype(jnp.float32)).reshape(B, T, C)
    y = y * scale.astype(jnp.float32)
    return y.astype(u.dtype), up[:, -POOL_PAST:]


def diff_attn_block(q, k, v, q_pos, k_pos, lam):
    s = jnp.einsum('bqhcd,bkhcd->bhcqk', q.astype(jnp.float32), k.astype(jnp.float32)) * (DIFF_HD ** -0.5)
    dist = (q_pos[:, None] - k_pos[None, :])
    bias = -alibi_slopes()[:, None, None] * dist.astype(jnp.float32)[None]
    s = jnp.where((dist >= 0)[None, None, None], s + bias[None, :, None], -jnp.inf)
    a = jax.nn.softmax(s, axis=-1)
    w = a[:, :, 0] - lam * a[:, :, 1]
    o = jnp.einsum('bhqk,bkhe->bqhe', w, v.astype(jnp.float32))
    return o.astype(v.dtype)


def diff_attn_prompt(q, k, v, lam):
    B, T = q.shape[0], q.shape[1]
    nb = T // Q_BLOCK
    qb = jnp.moveaxis(q.reshape(B, nb, Q_BLOCK, N_DIFF_HEADS, 2, DIFF_HD), 1, 0)
    k_pos = jnp.arange(T)

    def one_block(args):
        qi, i = args
        return diff_attn_block(qi, k, v, i * Q_BLOCK + jnp.arange(Q_BLOCK), k_pos, lam)

    o = lax.map(one_block, (qb, jnp.arange(nb)))
    return jnp.moveaxis(o, 0, 1).reshape(B, T, N_DIFF_HEADS, 2 * DIFF_HD)


def diff_attn_sample(q, k, v, lam, past_k, past_v):
    B, T = q.shape[0], q.shape[1]
    past = past_k.shape[1]
    kk = jnp.concatenate([past_k.astype(k.dtype), k.reshape(B, T, N_DIFF_HEADS, 2 * DIFF_HD)], axis=1)
    kk = kk.reshape(B, past + T, N_DIFF_HEADS, 2, DIFF_HD)
    vv = jnp.concatenate([past_v.astype(v.dtype), v], axis=1)
    return diff_attn_block(q, kk, vv, past + jnp.arange(T), jnp.arange(past + T), lam)


def cross_attn(h, mem_k, mem_v, wq, wo):
    B, T, _ = h.shape
    q = (h @ wq).reshape(B, T, N_CROSS_HEADS, CROSS_HD)
    s = jnp.einsum('bqhd,bkhd->bhqk', q.astype(jnp.float32), mem_k.astype(jnp.float32)) * (CROSS_HD ** -0.5)
    a = jax.nn.softmax(s, axis=-1)
    o = jnp.einsum('bhqk,bkhd->bqhd', a, mem_v.astype(jnp.float32)).reshape(B, T, D_MODEL)
    return o.astype(h.dtype) @ wo


def conv_ffn(h, prefix, w_up, conv_w, conv_b, w_down):
    T = h.shape[1]
    up = h @ w_up
    upp = jnp.concatenate([prefix.astype(up.dtype), up], axis=1)
    c = conv_b
    for j in range(CONV_W):
        c = c + upp[:, j:j + T] * conv_w[j]
    a, g = c[..., :D_FF], c[..., D_FF:]
    return (jax.nn.silu(g) * a) @ w_down, upp[:, -(CONV_W - 1):]


def trunk_layer(x, layer_idx, attend, start_pos, pool_prefix, conv_prefix, mem_k, mem_v,
                g_mix, w_in, pool_w, pool_scale, lam_q1, lam_k1, lam_q2, lam_k2, subln_g, w_out,
                g_cross, wq_c, wo_c, g_ffn, w_up, conv_w, conv_b, w_down):
    B, T, _ = x.shape
    h = rmsnorm(x, g_mix)
    z = h @ w_in
    u = z[..., :POOL_WIDTH]
    q = z[..., POOL_WIDTH:POOL_WIDTH + QK_WIDTH].reshape(B, T, N_DIFF_HEADS, 2, DIFF_HD)
    k = z[..., POOL_WIDTH + QK_WIDTH:POOL_WIDTH + 2 * QK_WIDTH].reshape(B, T, N_DIFF_HEADS, 2, DIFF_HD)
    v = z[..., POOL_WIDTH + 2 * QK_WIDTH:].reshape(B, T, N_DIFF_HEADS, 2 * DIFF_HD)
    pool_out, pool_state = pool_mix(u, pool_prefix, start_pos, pool_w, pool_scale)
    lam_0 = lambda_init(layer_idx)
    lam = (jnp.exp(jnp.sum(lam_q1.astype(jnp.float32) * lam_k1.astype(jnp.float32)))
           - jnp.exp(jnp.sum(lam_q2.astype(jnp.float32) * lam_k2.astype(jnp.float32))) + lam_0)
    o = attend(q, k, v, lam)
    o = rmsnorm(o, subln_g, SUBLN_EPS) * (1.0 - lam_0)
    mix = jnp.concatenate([pool_out, o.reshape(B, T, ATTN_WIDTH).astype(pool_out.dtype)], axis=-1)
    x = x + mix @ w_out
    x = x + cross_attn(rmsnorm(x, g_cross), mem_k, mem_v, wq_c, wo_c)
    f, conv_state = conv_ffn(rmsnorm(x, g_ffn), conv_prefix, w_up, conv_w, conv_b, w_down)
    x = x + f
    return x, k.reshape(B, T, N_DIFF_HEADS, 2 * DIFF_HD), v, pool_state, conv_state


def setup_inputs(seed: int = 0) -> dict:
    key = jax.random.key(seed)
    ks = jax.random.split(key, 40)
    n_pages = PAST_LEN // PAGE_SIZE
    n_used = DEC_BATCH * n_pages
    n_phys = n_used + n_used // 4

    def nrm(k, shape, s):
        return jax.random.normal(k, shape, jnp.float32) * s

    def gain(k, shape):
        return 1.0 + 0.1 * jax.random.normal(k, shape, jnp.float32)

    page_table = jax.random.permutation(ks[9], n_phys)[:n_used].reshape(DEC_BATCH, n_pages).astype(jnp.int32)
    return {
        'x_prompt': nrm(ks[0], (BATCH, SEQ, D_MODEL), 1.0),
        'x_sample': nrm(ks[1], (DEC_BATCH, DEC_SEQ, D_MODEL), 1.0),
        'mem_prompt': nrm(ks[2], (BATCH, N_MEM, D_MODEL), 1.0),
        'cache_k': nrm(ks[3], (DEPTH, n_phys, PAGE_SIZE, N_DIFF_HEADS, 2 * DIFF_HD), 1.0),
        'cache_v': nrm(ks[4], (DEPTH, n_phys, PAGE_SIZE, N_DIFF_HEADS, 2 * DIFF_HD), 1.0),
        'cache_mem_k': nrm(ks[5], (DEPTH, DEC_BATCH, N_MEM, N_CROSS_HEADS, CROSS_HD), 1.0),
        'cache_mem_v': nrm(ks[6], (DEPTH, DEC_BATCH, N_MEM, N_CROSS_HEADS, CROSS_HD), 1.0),
        'state_pool': nrm(ks[7], (DEPTH, DEC_BATCH, POOL_PAST, POOL_WIDTH), 1.0),
        'state_conv': nrm(ks[8], (DEPTH, DEC_BATCH, CONV_W - 1, 2 * D_FF), 1.0),
        'page_table': page_table,
        'g_mix': gain(ks[10], (DEPTH, D_MODEL)),
        'w_in': nrm(ks[11], (DEPTH, D_MODEL, IN_WIDTH), D_MODEL ** -0.5),
        'pool_w': nrm(ks[12], (DEPTH, N_POOL_GROUPS, POOL_GROUP, POOL_GROUP), POOL_GROUP ** -0.5),
        'pool_scale': gain(ks[13], (DEPTH, POOL_WIDTH)),
        'lam_q1': nrm(ks[14], (DEPTH, DIFF_HD), 0.1),
        'lam_k1': nrm(ks[15], (DEPTH, DIFF_HD), 0.1),
        'lam_q2': nrm(ks[16], (DEPTH, DIFF_HD), 0.1),
        'lam_k2': nrm(ks[17], (DEPTH, DIFF_HD), 0.1),
        'subln_g': gain(ks[18], (DEPTH, 2 * DIFF_HD)),
        'w_out': nrm(ks[19], (DEPTH, MIX_WIDTH, D_MODEL), MIX_WIDTH ** -0.5),
        'g_cross': gain(ks[20], (DEPTH, D_MODEL)),
        'wq_c': nrm(ks[21], (DEPTH, D_MODEL, D_MODEL), D_MODEL ** -0.5),
        'wk_c': nrm(ks[22], (DEPTH, D_MODEL, D_MODEL), D_MODEL ** -0.5),
        'wv_c': nrm(ks[23], (DEPTH, D_MODEL, D_MODEL), D_MODEL ** -0.5),
        'wo_c': nrm(ks[24], (DEPTH, D_MODEL, D_MODEL), D_MODEL ** -0.5),
        'g_ffn': gain(ks[25], (DEPTH, D_MODEL)),
        'w_up': nrm(ks[26], (DEPTH, D_MODEL, 2 * D_FF), D_MODEL ** -0.5),
        'conv_w': nrm(ks[27], (DEPTH, CONV_W, 2 * D_FF), 0.5),
        'conv_b': nrm(ks[28], (DEPTH, 2 * D_FF), 0.01),
        'w_down': nrm(ks[29], (DEPTH, D_FF, D_MODEL), D_FF ** -0.5),
        'g_final': gain(ks[30], (D_MODEL,)),
    }


def reference(x_prompt, x_sample, mem_prompt, cache_k, cache_v, cache_mem_k, cache_mem_v,
              state_pool, state_conv, page_table,
              g_mix, w_in, pool_w, pool_scale, lam_q1, lam_k1, lam_q2, lam_k2, subln_g, w_out,
              g_cross, wq_c, wk_c, wv_c, wo_c, g_ffn, w_up, conv_w, conv_b, w_down, g_final):
    Bp = x_prompt.shape[0]
    Bs = x_sample.shape[0]
    xp, xs = x_prompt, x_sample
    kp_l, vp_l, mkp_l, mvp_l, pp_l, cp_l = [], [], [], [], [], []
    ks_l, vs_l, ps_l, cs_l = [], [], [], []
    for l in range(DEPTH):
        lw = (g_mix[l], w_in[l], pool_w[l], pool_scale[l], lam_q1[l], lam_k1[l], lam_q2[l], lam_k2[l],
              subln_g[l], w_out[l], g_cross[l], wq_c[l], wo_c[l], g_ffn[l], w_up[l], conv_w[l], conv_b[l], w_down[l])
        mem_k = (mem_prompt @ wk_c[l]).reshape(Bp, N_MEM, N_CROSS_HEADS, CROSS_HD)
        mem_v = (mem_prompt @ wv_c[l]).reshape(Bp, N_MEM, N_CROSS_HEADS, CROSS_HD)
        pool0 = jnp.zeros((Bp, POOL_PAST, POOL_WIDTH), xp.dtype)
        conv0 = jnp.zeros((Bp, CONV_W - 1, 2 * D_FF), xp.dtype)
        xp, kp, vp, pp, cp = trunk_layer(xp, l, diff_attn_prompt, 0, pool0, conv0, mem_k, mem_v, *lw)
        kp_l.append(kp); vp_l.append(vp); mkp_l.append(mem_k); mvp_l.append(mem_v)
        pp_l.append(pp); cp_l.append(cp)
        past_k = cache_k[l][page_table].reshape(Bs, -1, N_DIFF_HEADS, 2 * DIFF_HD)
        past_v = cache_v[l][page_table].reshape(Bs, -1, N_DIFF_HEADS, 2 * DIFF_HD)
        attend_s = functools.partial(diff_attn_sample, past_k=past_k, past_v=past_v)
        xs, kn, vn, ps, cs = trunk_layer(xs, l, attend_s, past_k.shape[1], state_pool[l], state_conv[l],
                                         cache_mem_k[l], cache_mem_v[l], *lw)
        ks_l.append(kn); vs_l.append(vn); ps_l.append(ps); cs_l.append(cs)
    y_prompt = rmsnorm(xp, g_final)
    y_sample = rmsnorm(xs, g_final)
    return (y_prompt, y_sample,
            jnp.stack(kp_l), jnp.stack(vp_l), jnp.stack(mkp_l), jnp.stack(mvp_l),
            jnp.stack(pp_l), jnp.stack(cp_l),
            jnp.stack(ks_l), jnp.stack(vs_l), jnp.stack(ps_l), jnp.stack(cs_l))
```

```python
import math
import os
from contextlib import ExitStack
import numpy as np
import ml_dtypes
import concourse.bass as bass
import concourse.mybir as mybir
from concourse.bass_utils import run_bass_kernel_spmd

F32 = mybir.dt.float32
BF16 = mybir.dt.bfloat16
I32 = mybir.dt.int32
ALU = mybir.AluOpType
AF = mybir.ActivationFunctionType
AX = mybir.AxisListType

EPOCH = 30000
DEPTH = 2
NT = 16
TOK = 2176
HB = 2176
XC = 2208
DFF = 2816
NPAIR = 22
NROWS = 512 if os.environ.get("KSMALL") else 20480
NOAG = bool(os.environ.get("KNOAG"))


class Buf:
    __slots__ = ("name", "w", "r", "dsem", "dcnt")

    def __init__(self, name):
        self.name = name
        self.w = {}
        self.r = {}
        self.dsem = None
        self.dcnt = 0


def _merge(d, tok):
    k = id(tok[0])
    if k not in d or d[k][1] < tok[1]:
        d[k] = tok


class KB:
    def __init__(self, nc):
        self.nc = nc
        self.eng = {"pe": nc.tensor, "act": nc.scalar, "dve": nc.vector,
                    "pool": nc.gpsimd, "sp": nc.sync}
        self.esem = {e: nc.alloc_semaphore(name="es_%s_0" % e) for e in self.eng}
        self.ecnt = {e: 0 for e in self.eng}
        self.eep = {e: 0 for e in self.eng}
        self.seen = {e: {} for e in self.eng}
        self.dbufs = []
        self.dsems = {}
        self.dpool = []
        self.ninst = 0

    def _wait(self, e, tok):
        sem, val = tok
        k = id(sem)
        if self.seen[e].get(k, 0) >= val:
            return
        self.eng[e].wait_ge(sem, val)
        self.seen[e][k] = val

    def _deps(self, e, reads, writes):
        own = id(self.esem[e]) if e == "pe" else None
        for b in reads:
            for k, tok in b.w.items():
                if k != own:
                    self._wait(e, tok)
        for b in writes:
            for k, tok in b.w.items():
                if k != own:
                    self._wait(e, tok)
            for k, tok in b.r.items():
                if k != own:
                    self._wait(e, tok)

    def _mark(self, tok, reads, writes):
        for b in reads:
            _merge(b.r, tok)
        for b in writes:
            _merge(b.w, tok)
            b.r = {}

    def op(self, e, inst_fn, reads=(), writes=()):
        self._deps(e, reads, writes)
        if self.ecnt[e] >= EPOCH:
            self.eep[e] += 1
            self.esem[e] = self.nc.alloc_semaphore(name="es_%s_%d" % (e, self.eep[e]))
            self.ecnt[e] = 0
        inst = inst_fn()
        self.ecnt[e] += 1
        inst.then_inc(self.esem[e], 1)
        self._mark((self.esem[e], self.ecnt[e]), reads, writes)
        self.ninst += 1
        return inst

    def dma(self, q, dbuf, inst_fn, reads=(), writes=(), inc=16):
        self._deps(q, reads, writes)
        ent = self.dsems.get(dbuf.name)
        if ent is None and inc == 1:
            ent = [self.nc.alloc_semaphore(name="cs_" + dbuf.name), 0]
            self.dsems[dbuf.name] = ent
        if ent is None:
            if len(self.dpool) < 28:
                self.dpool.append([self.nc.alloc_semaphore(name="ds_%d" % len(self.dpool)), 0])
                ent = self.dpool[-1]
            else:
                ent = self.dpool[len(self.dsems) % 28]
            self.dsems[dbuf.name] = ent
        inst = inst_fn()
        ent[1] += inc
        inst.then_inc(ent[0], inc)
        self._mark((ent[0], ent[1]), reads, writes)
        self.ninst += 1
        return inst

    def barrier(self):
        toks = [(self.esem[f], self.ecnt[f]) for f in self.eng if self.ecnt[f] > 0]
        for ent in self.dpool:
            toks.append((ent[0], ent[1]))
        for e in self.eng:
            for t in toks:
                if t[0] is self.esem[e]:
                    continue
                self._wait(e, t)

    def wait_all(self, e, bufs):
        for b in bufs:
            for tok in list(b.w.values()) + list(b.r.values()):
                self._wait(e, tok)


def BC(ap, pos, n):
    lst = [list(x) for x in ap.ap]
    lst.insert(pos, [0, n])
    return bass.AP(ap.tensor, ap.offset, lst)


def PB(ap, n=128):
    lst = [[0, n]] + [list(x) for x in ap.ap]
    return bass.AP(ap.tensor, ap.offset, lst)


IN_SPECS = [
    ("xp", (2048, 1024), F32), ("xs", (128, 1024), F32), ("memp", (256, 1024), F32),
    ("ck0", (NROWS, 2048), F32), ("ck1", (NROWS, 2048), F32), ("cv0", (NROWS, 2048), F32), ("cv1", (NROWS, 2048), F32),
    ("cmk", (DEPTH, 128, 32768), F32), ("cmv", (DEPTH, 128, 32768), F32),
    ("spool", (DEPTH, 128, 7680), F32), ("sconv", (DEPTH, 128, 11264), F32),
    ("ptab", (128, 8), I32),
    ("g_mix", (DEPTH, 1024), F32), ("w_in", (DEPTH, 1024, 2048), F32), ("w_in_h", (DEPTH, 1024, 384), F32),
    ("pool_w", (DEPTH, 4, 128, 128), F32), ("pool_scale", (DEPTH, 512), F32),
    ("lam4", (DEPTH, 4, 64), F32), ("subln_g", (DEPTH, 128), F32),
    ("w_out", (DEPTH, 1024, 1024), F32), ("g_cross", (DEPTH, 1024), F32),
    ("wq_c", (DEPTH, 1024, 1024), F32), ("wq_ch", (DEPTH, 1024, 256), F32),
    ("wk_c", (DEPTH, 1024, 1024), F32), ("wv_c", (DEPTH, 1024, 1024), F32), ("wo_c", (DEPTH, 1024, 1024), F32),
    ("g_ffn", (DEPTH, 1024), F32), ("w_up", (DEPTH, 1024, 5632), F32),
    ("conv_w", (DEPTH, 3, 5632), F32), ("conv_b", (DEPTH, 5632), F32),
    ("w_down", (DEPTH, 2816, 1024), F32), ("g_final", (1024,), F32),
    ("ident", (128, 128), F32), ("sel", (128, 4), F32), ("maskd", (128, 4, 128), F32),
    ("alibi_p", (128, 256), F32), ("alibi_s", (128, 1025), F32), ("isnew", (128, 1), F32),
    ("invc0", (128, 4, 128), F32), ("wtab", (128, 16), F32),
]
OUT_SPECS = [
    ("y_p", (2048, 1024)), ("y_s", (128, 1024)), ("k_p", (DEPTH, 2048, 512)), ("v_p", (DEPTH, 2048, 512)),
    ("memk", (DEPTH, 256, 1024)), ("memv", (DEPTH, 256, 1024)), ("pool_p", (DEPTH, 15, 512)),
    ("conv_p", (DEPTH, 2, 5632)), ("k_s", (DEPTH, 128, 512)), ("v_s", (DEPTH, 128, 512)),
    ("pool_s", (DEPTH, 128, 7680)), ("conv_s", (DEPTH, 128, 11264)),
]


def build_program():
    nc = bass.Bass("TRN2", target_bir_lowering=False, num_devices=8)
    kb = KB(nc)
    STOP = float(os.environ.get("KSTOP", "99"))

    class _Stop(Exception):
        pass

    def ck(x):
        if STOP <= x:
            raise _Stop()
    I = {n: nc.dram_tensor(n, list(s), d, kind="ExternalInput").ap() for n, s, d in IN_SPECS}
    O = {n: nc.dram_tensor(n, list(s), F32, kind="ExternalOutput").ap() for n, s in OUT_SPECS}
    bout = Buf("out")

    def dint(name, shape, dt):
        return nc.dram_tensor(name, list(shape), dt, kind="Internal").ap()

    def sb(name, shape, dt=F32):
        return nc.alloc_sbuf_tensor("s_" + name, list(shape), dt)

    uid = {"i": 0}

    def tb(es, name, shape, dt=F32):
        uid["i"] += 1
        return es.enter_context(nc.sbuf_tensor("t_%s_%d" % (name, uid["i"]), list(shape), dt))

    G4 = [[0, 1, 2, 3], [4, 5, 6, 7]]
    G8 = [list(range(8))]

    psl = [nc.alloc_psum_tensor("ps%d" % i, [128, 512], F32) for i in range(7)]
    psb = nc.alloc_psum_tensor("psb", [128, 1024], BF16)
    bps = [Buf("ps%d" % i) for i in range(7)]
    bpsb = Buf("psb")
    rr = {"ps": 0, "ev": 0, "n": 7}

    def nps():
        i = rr["ps"] % rr["n"]
        rr["ps"] = i + 1
        return psl[i], bps[i]

    def evac(out_ap, in_ap, reads, writes, eng=None):
        if eng is None:
            eng = ("act", "dve")[rr["ev"] % 2]
            rr["ev"] += 1
        if eng == "act":
            kb.op("act", lambda: nc.scalar.activation(out=out_ap, in_=in_ap, func=AF.Copy), reads, writes)
        else:
            kb.op(eng, lambda: kb.eng[eng].tensor_copy(out=out_ap, in_=in_ap), reads, writes)

    def mm(ps_ap, lhsT, rhs, start, stop, reads, writes):
        kb.op("pe", lambda: nc.tensor.matmul(ps_ap, lhsT=lhsT, rhs=rhs, start=start, stop=stop), reads, writes)

    def tr(ps_ap, in_ap, idt, reads, writes):
        kb.op("pe", lambda: nc.tensor.transpose(ps_ap, in_ap, idt), reads, writes)

    def ld(q, dbuf, out_ap, in_ap, reads=(), writes=(), slow=False):
        e = kb.eng[q]
        if slow:
            kb.dma(q, dbuf, lambda: e.dma_start(out=out_ap, in_=in_ap, allow_slow_non_contiguous=True), reads, writes)
        else:
            kb.dma(q, dbuf, lambda: e.dma_start(out=out_ap, in_=in_ap), reads, writes)

    def V(fn, *a, **k):
        return lambda: fn(*a, **k)

    xT = sb("xT", [128, 8, XC])
    bx = [Buf("x%d" % t) for t in range(17)]
    bxh = Buf("xhalo")
    hTt = sb("hTt", [128, 8, 128], BF16); bht = Buf("hTt")
    ident_f = sb("ident_f", [128, 128]); bidf = Buf("idf")
    ident_b = sb("ident_b", [128, 128], BF16); bidb = Buf("idb")
    ones_f = sb("ones_f", [128, 128]); bones = Buf("ones")
    cst = sb("cst", [128, 4]); bcst = Buf("cst")
    gvec = sb("gvec", [128, DEPTH, 3, 8]); bgv = Buf("gvec")
    gfin = sb("gfin", [128, 8]); bgf = Buf("gfin")
    psc = sb("psc", [128, DEPTH, 4]); bpsc = Buf("psc")
    cw = sb("cw", [128, 3, 44]); bcw = Buf("cw")
    cbv = sb("cbv", [128, 44]); bcb = Buf("cb")
    gsub = sb("gsub", [128, 128]); bgsub = Buf("gsub")
    lamv = sb("lamv", [128, 4, 64]); blamv = Buf("lamv")
    lamt = sb("lamt", [128, 8]); blam = Buf("lam")
    selt = sb("selt", [128, 4]); bsel = Buf("sel")
    maskt = sb("maskt", [128, 4, 128], BF16); bmask = Buf("mask")
    alip = sb("alip", [128, 256]); balip = Buf("alip")
    isnew = sb("isnew", [128, 1]); bisn = Buf("isnew")
    invc0 = sb("invc0", [128, 4, 128]); binv = Buf("invc0")
    ptab = sb("ptab", [128, 8], I32); bptab = Buf("ptab")
    mpT = sb("mpT", [128, 8, 256], BF16); bmpT = Buf("mpT")
    stg = sb("stg", [128, 2, 1024]); bstg = [Buf("stg0"), Buf("stg1")]
    sq = sb("sq", [128, 8, 128]); bsq = Buf("sq")
    rstd = sb("rstd", [128, 128]); brs = Buf("rstd")
    small = sb("small", [128, 16]); bsm = Buf("small")
    otmp = sb("otmp", [128, 2, 128]); bot = Buf("otmp")
    stgc = {"i": 0}

    def nstg():
        i = stgc["i"]
        stgc["i"] = 1 - i
        return stg[:, i, :], bstg[i]

    ld("sp", bidf, ident_f[:], I["ident"], writes=[bidf])
    ld("pool", bidb, ident_b[:], I["ident"], writes=[bidb])
    kb.op("pool", V(nc.gpsimd.memset, ones_f[:], 1.0), writes=[bones])
    kb.op("pool", V(nc.gpsimd.memset, cst[:, 0:1], 1e-6), writes=[bcst])
    kb.op("pool", V(nc.gpsimd.memset, cst[:, 1:2], 1e-5), writes=[bcst])
    kb.op("pool", V(nc.gpsimd.memset, cst[:, 2:3], 0.0), writes=[bcst])
    for l in range(DEPTH):
        for i, nm in enumerate(("g_mix", "g_cross", "g_ffn")):
            ld("sp", bgv, gvec[:, l, i, :], I[nm][l].rearrange("(kc p) -> p kc", p=128), writes=[bgv], slow=True)
        ld("sp", bpsc, psc[:, l, :], I["pool_scale"][l].rearrange("(kc p) -> p kc", p=128), writes=[bpsc], slow=True)
    ld("sp", bgf, gfin[:], I["g_final"].rearrange("(kc p) -> p kc", p=128), writes=[bgf], slow=True)
    ld("sp", bsel, selt[:], I["sel"], writes=[bsel])
    ld("pool", bmask, maskt[:], I["maskd"], writes=[bmask])
    ld("sp", balip, alip[:], I["alibi_p"], writes=[balip])
    ld("sp", bisn, isnew[:], I["isnew"], writes=[bisn])
    ld("sp", binv, invc0[:], I["invc0"], writes=[binv])
    wtab = sb("wtab", [128, 16]); bwt = Buf("wtab")
    ld("sp", bwt, wtab[:], I["wtab"], writes=[bwt])
    ld("sp", bptab, ptab[:], I["ptab"], writes=[bptab])
    idxs = sb("idxs", [128, 8, 8], I32); bidx = Buf("idxs")
    for qt in range(8):
        kb.op("dve", V(nc.vector.tensor_scalar, out=idxs[:, :, qt], in0=ptab[:, :], scalar1=8, scalar2=qt, op0=ALU.mult, op1=ALU.add),
              reads=[bptab], writes=[bidx])

    def norm(c0, n, gap, dst_fn, xbufs, dbufs, epscol=0):
        kb.op("act", V(nc.scalar.activation, out=sq[:, :, 0:n], in_=xT[:, :, c0:c0 + n], func=AF.Square),
              reads=xbufs, writes=[bsq])
        ps, bp = nps()
        for kc in range(8):
            mm(ps[:, 0:n], ones_f[:], sq[:, kc, 0:n], kc == 0, kc == 7, [bones, bsq], [bp])
        kb.op("act", V(nc.scalar.activation, out=rstd[:, 0:n], in_=ps[:, 0:n], func=AF.Sqrt,
                       bias=cst[:, epscol:epscol + 1], scale=1.0 / 1024.0), reads=[bp, bcst], writes=[brs])
        kb.op("dve", V(nc.vector.reciprocal, out=rstd[:, 0:n], in_=rstd[:, 0:n]), reads=[brs], writes=[brs])
        for kc in range(8):
            kb.op("dve", V(nc.vector.scalar_tensor_tensor, out=dst_fn(kc), in0=xT[:, kc, c0:c0 + n], scalar=gap[:, kc:kc + 1],
                           in1=rstd[:, 0:n], op0=ALU.mult, op1=ALU.mult), reads=list(xbufs) + [brs, bgv, bgf], writes=dbufs)

    def norm_tile(t, gap):
        norm(t * 128, 128, gap, lambda kc: hTt[:, kc, :], [bx[t]], [bht])

    def transpose_in(src_ap, dst_fn, bdst):
        st, bs = nstg()
        ld("sp", bs, st, src_ap, writes=[bs])
        for half in range(2):
            ps, bp = nps()
            for k4 in range(4):
                kc = half * 4 + k4
                tr(ps[:, k4 * 128:(k4 + 1) * 128], st[:, kc * 128:(kc + 1) * 128], ident_f[:], [bs, bidf], [bp])
            evac(dst_fn(half), ps[:].rearrange("p (k n) -> p k n", k=4), [bp], [bdst])

    for t in range(17):
        src = I["xp"][t * 128:(t + 1) * 128, :] if t < 16 else I["xs"]
        transpose_in(src, lambda half, t=t: xT[:, half * 4:half * 4 + 4, t * 128:(t + 1) * 128], bx[t])
    for mh in range(2):
        transpose_in(I["memp"][mh * 128:(mh + 1) * 128, :],
                     lambda half, mh=mh: mpT[:, half * 4:half * 4 + 4, mh * 128:(mh + 1) * 128], bmpT)

    sendK = [dint("sendK%d" % i, [256, 2048], BF16) for i in range(2)]
    recvK = [dint("recvK%d" % i, [1024, 2048], BF16) for i in range(2)]
    sendV = [dint("sendV%d" % i, [1024, 512], BF16) for i in range(2)]
    recvV = [dint("recvV%d" % i, [4096, 512], BF16) for i in range(2)]
    sendU = dint("sendU", [128, 960], F32); recvU = dint("recvU", [512, 960], F32)
    sendX = dint("sendX", [128, 256], F32); recvX = dint("recvX", [512, 256], F32)
    sendS = dint("sendS", [128, 258], F32); recvS = dint("recvS", [512, 258], F32)
    sendC = dint("sendC", [128, 257], F32); recvC = dint("recvC", [512, 257], F32)
    bsU, brU, bsX, brX, bsS, brS, bsC, brC = [Buf("cc%d" % i) for i in range(8)]
    bsK = [Buf("sK0"), Buf("sK1")]; brK = [Buf("rK0"), Buf("rK1")]
    bsV = [Buf("sV0"), Buf("sV1")]; brV = [Buf("rV0"), Buf("rV1")]

    def allgather(send, recv, bs_, br_, groups):
        if NOAG:
            return
        kb.dma("pool", br_, V(nc.gpsimd.collective_compute, "AllGather", ALU.bypass, replica_groups=groups,
                              ins=[send], outs=[recv]), reads=[bs_], writes=[br_], inc=1)

    def wload(dst_ap, src_ap, wbuf):
        ld("pool", wbuf, dst_ap, src_ap, writes=[wbuf])

    def wmat(dst, name, l, wbuf, c0=0, c1=None):
        src = I[name][l]
        if c1 is not None:
            src = src[:, c0:c1]
        wload(dst, src.rearrange("(kc p) f -> p kc f", p=128), wbuf)

    def diff_combine(o1, o2, l1, l2, dst, rd, wr, l):
        kb.op("dve", V(nc.vector.reciprocal, out=small[:, 0:1], in_=l1), reads=rd, writes=[bsm])
        kb.op("dve", V(nc.vector.reciprocal, out=small[:, 1:2], in_=l2), reads=list(rd) + [bsm], writes=[bsm])
        kb.op("dve", V(nc.vector.tensor_tensor, out=small[:, 1:2], in0=small[:, 1:2], in1=lamt[:, 4 + l:5 + l], op=ALU.mult),
              reads=[bsm, blam], writes=[bsm])
        kb.op("dve", V(nc.vector.tensor_scalar, out=otmp[:, 0, :], in0=o1, scalar1=small[:, 0:1], scalar2=None, op0=ALU.mult),
              reads=list(rd) + [bsm], writes=[bot])
        kb.op("dve", V(nc.vector.scalar_tensor_tensor, out=otmp[:, 0, :], in0=o2, scalar=small[:, 1:2], in1=otmp[:, 0, :],
                       op0=ALU.mult, op1=ALU.add), reads=list(rd) + [bsm, bot], writes=[bot])
        kb.op("dve", V(nc.vector.memset, small[:, 2:3], 0.0), writes=[bsm])
        kb.op("act", V(nc.scalar.activation, out=otmp[:, 1, :], in_=otmp[:, 0, :], func=AF.Square, accum_out=small[:, 2:3]),
              reads=[bot, bsm], writes=[bot, bsm])
        kb.op("act", V(nc.scalar.activation, out=small[:, 3:4], in_=small[:, 2:3], func=AF.Sqrt, bias=cst[:, 1:2], scale=1.0 / 128.0),
              reads=[bsm, bcst], writes=[bsm])
        kb.op("dve", V(nc.vector.reciprocal, out=small[:, 3:4], in_=small[:, 3:4]), reads=[bsm], writes=[bsm])
        kb.op("dve", V(nc.vector.scalar_tensor_tensor, out=dst, in0=otmp[:, 0, :], scalar=small[:, 3:4], in1=gsub[:],
                       op0=ALU.mult, op1=ALU.mult), reads=[bot, bsm, bgsub], writes=wr)

    def halo_select(es, recv, brecv, ncols, inner, name):
        acc = tb(es, name + "a", [128, ncols]); tmp = tb(es, name + "t", [128, ncols])
        bacc_, btmp = Buf(name + "a"), Buf(name + "t")
        kb.op("dve", V(nc.vector.memset, acc[:], 0.0), writes=[bacc_])
        for r in range(4):
            ld("sp", btmp, tmp[:], recv[r * 128:(r + 1) * 128, :], reads=[brecv], writes=[btmp])
            if r < 3:
                kb.op("dve", V(nc.vector.scalar_tensor_tensor, out=acc[:], in0=tmp[:], scalar=selt[:, r:r + 1], in1=acc[:],
                               op0=ALU.mult, op1=ALU.add), reads=[btmp, bsel, bacc_], writes=[bacc_])
            else:
                a4 = acc[:].rearrange("p (c t n) -> p c t n", t=16, n=inner)
                t4 = tmp[:].rearrange("p (c t n) -> p c t n", t=16, n=inner)
                nch = ncols // (16 * inner)
                for c_ in range(nch):
                    kb.op("dve", V(nc.vector.scalar_tensor_tensor, out=a4[:, c_, 1:16, :], in0=t4[:, c_, 0:15, :], scalar=selt[:, 3:4],
                                   in1=a4[:, c_, 1:16, :], op0=ALU.mult, op1=ALU.add), reads=[btmp, bsel, bacc_], writes=[bacc_])
        return acc, bacc_

    try:
        for l in range(DEPTH):
            if STOP <= 0:
                break
            lam0 = 0.8 - 0.6 * math.exp(-0.3 * l)
            ld("sp", blamv, lamv[:], PB(I["lam4"][l]), writes=[blamv])
            kb.op("dve", V(nc.vector.tensor_tensor, out=lamv[:, 0, :], in0=lamv[:, 0, :], in1=lamv[:, 1, :], op=ALU.mult), reads=[blamv], writes=[blamv])
            kb.op("dve", V(nc.vector.tensor_tensor, out=lamv[:, 2, :], in0=lamv[:, 2, :], in1=lamv[:, 3, :], op=ALU.mult), reads=[blamv], writes=[blamv])
            kb.op("dve", V(nc.vector.tensor_reduce, out=lamt[:, 0:1], in_=lamv[:, 0, :], axis=AX.X, op=ALU.add), reads=[blamv], writes=[blam])
            kb.op("dve", V(nc.vector.tensor_reduce, out=lamt[:, 1:2], in_=lamv[:, 2, :], axis=AX.X, op=ALU.add), reads=[blamv], writes=[blam])
            kb.op("act", V(nc.scalar.activation, out=lamt[:, 0:2], in_=lamt[:, 0:2], func=AF.Exp), reads=[blam], writes=[blam])
            kb.op("dve", V(nc.vector.tensor_tensor, out=lamt[:, 2:3], in0=lamt[:, 1:2], in1=lamt[:, 0:1], op=ALU.subtract), reads=[blam], writes=[blam])
            kb.op("dve", V(nc.vector.tensor_scalar, out=lamt[:, 4 + l:5 + l], in0=lamt[:, 2:3], scalar1=-lam0, scalar2=None, op0=ALU.add),
                  reads=[blam], writes=[blam])
            ld("sp", bgsub, gsub[:], PB(I["subln_g"][l]), writes=[bgsub])
            kb.op("dve", V(nc.vector.tensor_scalar, out=gsub[:], in0=gsub[:], scalar1=1.0 - lam0, scalar2=None, op0=ALU.mult), reads=[bgsub], writes=[bgsub])
            ld("sp", bcw, cw[:], I["conv_w"][l].rearrange("j (ch p) -> p j ch", p=128), writes=[bcw], slow=True)
            ld("sp", bcb, cbv[:], I["conv_b"][l].rearrange("(ch p) -> p ch", p=128), writes=[bcb], slow=True)

            with ExitStack() as front:
                uT = tb(front, "uT", [128, 4, 2048], BF16); buT = [Buf("u%d" % t) for t in range(16)]
                qT = tb(front, "qT", [128, 4, 2048], BF16); bq = [Buf("q%d" % t) for t in range(16)]
                boN = [Buf("oN%d" % t) for t in range(17)]
                zs = tb(front, "zs", [128, 2048]); bzs = Buf("zs")
                zsh = tb(front, "zsh", [128, 384]); bzsh = Buf("zsh")

                with ExitStack() as es:
                    wA = tb(es, "wA", [128, 8, 2048], BF16); bwA = Buf("wA")
                    wh = tb(es, "wh", [128, 8, 384], BF16); bwh = Buf("wh")
                    kTs = tb(es, "kTs", [128, 4, 256], BF16); bkTs = Buf("kTs")
                    vst = tb(es, "vst", [128, 512], BF16); bvst = Buf("vst")
                    utl = tb(es, "utl", [128, 4, 16, 15]); butl = Buf("utl")
                    for q4 in range(4):
                        wmat(wA[:, :, q4 * 512:(q4 + 1) * 512], "w_in", l, bwA, q4 * 512, (q4 + 1) * 512)
                    wmat(wh[:], "w_in_h", l, bwh)
                    gm = gvec[:, l, 0, :]
                    for t in range(16):
                        norm_tile(t, gm)
                        hsl = lambda kc: hTt[:, kc, :]
                        for ch in range(12):
                            ps, bp = nps()
                            for kc in range(8):
                                mm(ps[:, 0:128], wA[:, kc, ch * 128:(ch + 1) * 128], hsl(kc), kc == 0, kc == 7, [bwA, bht], [bp])
                            if ch < 4:
                                evac(uT[:, ch, t * 128:(t + 1) * 128], ps[:, 0:128], [bp], [buT[t]], eng="act")
                                evac(utl[:, ch, t, :], ps[:, 113:128], [bp], [butl], eng="act")
                            elif ch < 8:
                                evac(qT[:, ch - 4, t * 128:(t + 1) * 128], ps[:, 0:128], [bp], [bq[t]])
                            else:
                                evac(kTs[:, ch - 8, (t % 2) * 128:(t % 2 + 1) * 128], ps[:, 0:128], [bp], [bkTs])
                        if t % 2 == 1:
                            t0 = (t // 2) * 256
                            for hp in range(2):
                                ld("sp", bkTs, sendK[hp].rearrange("(h p) n -> p h n", p=128)[:, :, t0:t0 + 256], kTs[:, 2 * hp:2 * hp + 2, :],
                                   reads=[bkTs], writes=[bsK[hp]])
                        st, bs = nstg()
                        for kv in range(2):
                            ps, bp = nps()
                            for kc in range(8):
                                mm(ps[:], hsl(kc), wA[:, kc, 1024 + kv * 512:1536 + kv * 512], kc == 0, kc == 7, [bwA, bht], [bp])
                            evac(st[:, kv * 512:(kv + 1) * 512], ps[:], [bp], [bs], eng="dve")
                            if kv == 1:
                                evac(vst[:], ps[:], [bp], [bvst], eng="dve")
                        ld("sp", bs, O["k_p"][l, t * 128:(t + 1) * 128, :], st[:, 0:512], reads=[bs], writes=[bout])
                        ld("sp", bs, O["v_p"][l, t * 128:(t + 1) * 128, :], st[:, 512:1024], reads=[bs], writes=[bout])
                        ld("sp", bvst, sendV[t // 8][(t % 8) * 128:(t % 8 + 1) * 128, :], vst[:], reads=[bvst], writes=[bsV[t // 8]])
                    norm_tile(16, gm)
                    for c4 in range(4):
                        ps, bp = nps()
                        for kc in range(8):
                            mm(ps[:], hTt[:, kc, :], wA[:, kc, c4 * 512:(c4 + 1) * 512], kc == 0, kc == 7, [bwA, bht], [bp])
                        evac(zs[:, c4 * 512:(c4 + 1) * 512], ps[:], [bp], [bzs])
                    ps, bp = nps()
                    for kc in range(8):
                        mm(ps[:, 0:384], hTt[:, kc, :], wh[:, kc, :], kc == 0, kc == 7, [bwh, bht], [bp])
                    evac(zsh[:], ps[:, 0:384], [bp], [bzsh])
                    ld("sp", bzs, O["k_s"][l], zs[:, 1024:1536], reads=[bzs], writes=[bout])
                    ld("sp", bzs, O["v_s"][l], zs[:, 1536:2048], reads=[bzs], writes=[bout])
                    ld("sp", bzs, O["pool_s"][l][:, 14 * 512:15 * 512], zs[:, 0:512], reads=[bzs], writes=[bout])
                    ld("sp", bout, O["pool_s"][l][:, 0:14 * 512], I["spool"][l][:, 512:15 * 512], writes=[bout])
                    for ch in range(4):
                        ld("sp", butl, O["pool_p"][l][:, ch * 128:(ch + 1) * 128].rearrange("t p -> p t"), utl[:, ch, 15, :],
                           reads=[butl], writes=[bout], slow=True)
                    ld("sp", butl, sendU, utl[:].rearrange("p c t n -> p (c t n)"), reads=[butl], writes=[bsU])
                    for i2 in range(2):
                        allgather(sendK[i2], recvK[i2], bsK[i2], brK[i2], G4)
                        allgather(sendV[i2], recvV[i2], bsV[i2], brV[i2], G4)
                    allgather(sendU, recvU, bsU, brU, G4)
                    kb.barrier()
                if STOP <= 1:
                    break

                with ExitStack() as es:
                    kq = tb(es, "kq", [128, 4, 2048]); bkq = [Buf("kq%d" % i) for i in range(4)]
                    prod = tb(es, "prod", [128, 2, 2048]); bprod = [Buf("prod0"), Buf("prod1")]
                    Ssc = tb(es, "Ssc", [128, 1025, 2]); bS = Buf("S")
                    alis = tb(es, "alis", [128, 1025]); balis = Buf("alis")
                    accs = tb(es, "accs", [128, 258]); bacc = Buf("accs")
                    part = tb(es, "part", [128, 2, 128]); bpart = Buf("part")
                    ld("sp", balis, alis[:], I["alibi_s"], writes=[balis])
                    qh_b = BC(zsh[:, 0:128], 1, 16)
                    tasks = [("k", sl, qt) for sl in range(8) for qt in range(8)] + [("v", sl, qt) for sl in range(8) for qt in range(8)]
                    DPF = 3

                    def gather(ti):
                        kind, sl, qt = tasks[ti]
                        i = ti % 4
                        kb.dma("pool", bkq[i], V(nc.gpsimd.indirect_dma_start, out=kq[:, i, :], out_offset=None,
                                                 in_=I[("ck%d" if kind == "k" else "cv%d") % l],
                                                 in_offset=bass.IndirectOffsetOnAxis(ap=idxs[:, sl, qt:qt + 1], axis=0)),
                               reads=[bidx], writes=[bkq[i]])

                    for ti in range(DPF):
                        gather(ti)
                    for ti in range(64):
                        kind, sl, qt = tasks[ti]
                        i = ti % 4
                        j2 = ti % 2
                        if ti + DPF < len(tasks):
                            gather(ti + DPF)
                        kb.op("pool", V(nc.gpsimd.tensor_tensor, out=prod[:, j2, :].rearrange("p (n d) -> p n d", d=128),
                                        in0=kq[:, i, :].rearrange("p (n d) -> p n d", d=128), in1=qh_b, op=ALU.mult),
                              reads=[bkq[i], bzsh], writes=[bprod[j2]])
                        p0 = sl * 128 + qt * 16
                        kb.op("dve", V(nc.vector.tensor_reduce, out=Ssc[:, p0:p0 + 16, :],
                                       in_=prod[:, j2, :].rearrange("p (n c d) -> p n c d", c=2, d=64), axis=AX.X, op=ALU.add),
                              reads=[bprod[j2]], writes=[bS])
                    kb.op("pool", V(nc.gpsimd.tensor_tensor, out=prod[:, 0, 0:128], in0=zsh[:, 0:128], in1=zsh[:, 128:256], op=ALU.mult),
                          reads=[bzsh], writes=[bprod[0]])
                    kb.op("dve", V(nc.vector.tensor_reduce, out=Ssc[:, 1024, :], in_=prod[:, 0, 0:128].rearrange("p (c d) -> p c d", c=2),
                                   axis=AX.X, op=ALU.add), reads=[bprod[0]], writes=[bS])
                    kb.op("dve", V(nc.vector.scalar_tensor_tensor, out=Ssc[:], in0=Ssc[:], scalar=0.125, in1=BC(alis[:], 2, 2),
                                   op0=ALU.mult, op1=ALU.add), reads=[bS, balis], writes=[bS])
                    kb.op("act", V(nc.scalar.activation, out=Ssc[:], in_=Ssc[:], func=AF.Exp), reads=[bS], writes=[bS])
                    kb.op("dve", V(nc.vector.tensor_scalar, out=Ssc[:, 1024, :], in0=Ssc[:, 1024, :], scalar1=isnew[:, 0:1], scalar2=None,
                                   op0=ALU.mult), reads=[bS, bisn], writes=[bS])
                    for c in range(2):
                        kb.op("dve", V(nc.vector.tensor_reduce, out=accs[:, c * 129 + 128:c * 129 + 129], in_=Ssc[:, :, c], axis=AX.X, op=ALU.add),
                              reads=[bS], writes=[bacc])
                        kb.op("dve", V(nc.vector.tensor_scalar, out=accs[:, c * 129:c * 129 + 128], in0=zsh[:, 256:384],
                                       scalar1=Ssc[:, 1024, c:c + 1], scalar2=None, op0=ALU.mult), reads=[bS, bzsh], writes=[bacc])
                    for ti in range(64, 128):
                        kind, sl, qt = tasks[ti]
                        i = ti % 4
                        if ti + DPF < len(tasks):
                            gather(ti + DPF)
                        p0 = sl * 128 + qt * 16
                        for c in range(2):
                            kb.op("pool", V(nc.gpsimd.tensor_tensor, out=prod[:, c, :].rearrange("p (n d) -> p n d", d=128),
                                            in0=kq[:, i, :].rearrange("p (n d) -> p n d", d=128),
                                            in1=BC(Ssc[:, p0:p0 + 16, c], 2, 128), op=ALU.mult), reads=[bkq[i], bS], writes=[bprod[c]])
                            kb.op("dve", V(nc.vector.tensor_reduce, out=part[:, c, :], in_=prod[:, c, :].rearrange("p (n d) -> p d n", d=128),
                                           axis=AX.X, op=ALU.add), reads=[bprod[c]], writes=[bpart])
                            kb.op("dve", V(nc.vector.tensor_tensor, out=accs[:, c * 129:c * 129 + 128], in0=accs[:, c * 129:c * 129 + 128],
                                           in1=part[:, c, :], op=ALU.add), reads=[bpart, bacc], writes=[bacc])
                    ld("sp", bacc, sendS, accs[:], reads=[bacc], writes=[bsS])
                    allgather(sendS, recvS, bsS, brS, G4)
                    kb.barrier()
                if STOP <= 2:
                    break

                front2 = front.enter_context(ExitStack())
                oN = tb(front2, "oN", [128, 17, 512], BF16)
                with ExitStack() as es:
                    rr["n"] = 3
                    rr["ps"] = 0
                    kTh_ = tb(es, "kTh", [128, 64, 128], BF16); bkv = Buf("kvb")
                    vh_ = tb(es, "vh", [128, 64, 130], BF16)
                    kTh = kTh_[:]
                    vh = vh_[:]
                    NSL = 6
                    LOOK = 3
                    pT_ = tb(es, "pT", [128, NSL, 4, 128], BF16)
                    bpT = [[Buf("pT%d_%d" % (a, b_)) for b_ in range(4)] for a in range(NSL)]
                    kb.op("pool", V(nc.gpsimd.memset, vh_[:, :, 128:130], 1.0), writes=[bkv])
                    for h in range(4):
                        for r in range(4):
                            ld("sp", bkv, kTh.rearrange("p (m r) n -> p r m n", r=4)[:, r],
                               recvK[h // 2][r * 256 + (h % 2) * 128:r * 256 + (h % 2 + 1) * 128, :].rearrange("p (m n) -> p m n", n=128),
                               reads=[brK[h // 2]], writes=[bkv])
                            for th in range(2):
                                ld("sp", bkv, vh.rearrange("p (m r) n -> p r m n", r=4)[:, r, 8 * th:8 * th + 8, 0:128],
                                   recvV[th][r * 1024:(r + 1) * 1024, h * 128:(h + 1) * 128].rearrange("(m p) e -> p m e", p=128),
                                   reads=[brV[th]], writes=[bkv])
                        kb.op("pool", V(nc.gpsimd.memset, vh_[:, :, 128:130], 1.0), writes=[bkv])
                        if h >= 1:
                            wb = BC(BC(wtab[:, h * 4:(h + 1) * 4], 1, 16), 3, 130)
                            vh4 = vh_[:].rearrange("p (g i) n -> p g i n", i=4)
                            kb.op("pool", V(nc.gpsimd.tensor_tensor, out=vh4, in0=vh4, in1=wb, op=ALU.mult), reads=[bkv, bwt], writes=[bkv])
                        glist = [(m, g0, c) for m in range(16) for g0 in range(0, 4 * m + 4, 4) for c in range(2)]
                        slope_h = 2.0 ** (-2.0 * (h + 1))

                        def emit_scores(gi):
                            m, g0, c = glist[gi]
                            sl = gi % NSL
                            ps, bp = nps()
                            for i4 in range(4):
                                kt = g0 + i4
                                mm(ps[:, i4 * 128:(i4 + 1) * 128], kTh[c * 64:(c + 1) * 64, kt, :],
                                   qT[c * 64:(c + 1) * 64, h, m * 128:(m + 1) * 128], True, True, [bkv, bq[m]], [bp])
                            if h >= 1:
                                kb.op("act", V(nc.scalar.activation, out=pT_[:, sl].rearrange("p i n -> p (i n)"), in_=ps[:, 0:512], func=AF.Exp,
                                               bias=float(slope_h * 128.0 * (g0 - 4 * m)), scale=0.125), reads=[bp], writes=bpT[sl])
                            for i4 in range(4):
                                kt = g0 + i4
                                e = 4 * m + 3 - kt
                                if h == 0:
                                    kb.op("act", V(nc.scalar.activation, out=pT_[:, sl, i4, :], in_=ps[:, i4 * 128:(i4 + 1) * 128], func=AF.Exp,
                                                   bias=alip[:, h * 64 + e:h * 64 + e + 1], scale=0.125), reads=[bp, balip], writes=[bpT[sl][i4]])
                                if e < 4:
                                    kb.op("dve", V(nc.vector.tensor_tensor, out=pT_[:, sl, i4, :], in0=pT_[:, sl, i4, :], in1=maskt[:, e, :],
                                                   op=ALU.mult), reads=[bpT[sl][i4], bmask], writes=[bpT[sl][i4]])

                        def emit_pv(gi):
                            m, g0, c = glist[gi]
                            sl = gi % NSL
                            nk = 4 * m + 4
                            pi = 3 + 2 * (m % 2)
                            pso, bpo = psl[pi + c], bps[pi + c]
                            for i4 in range(4):
                                kt = g0 + i4
                                mm(pso[:, 0:129], pT_[:, sl, i4, :], vh[:, kt, 0:129], kt == 0, kt == nk - 1, [bpT[sl][i4], bkv], [bpo])
                            if g0 + 4 == nk and c == 1:
                                p1, b1, p2, b2 = psl[pi], bps[pi], psl[pi + 1], bps[pi + 1]
                                diff_combine(p1[:, 0:128], p2[:, 0:128], p1[:, 128:129], p2[:, 128:129],
                                             oN[:, m, h * 128:(h + 1) * 128], [b1, b2], [boN[m]], l)

                        for gi in range(len(glist) + LOOK):
                            if gi < len(glist):
                                emit_scores(gi)
                            if gi >= LOOK:
                                emit_pv(gi - LOOK)
                    rr["n"] = 7
                    kb.barrier()
                if STOP <= 3:
                    break

                with ExitStack() as es:
                    Rm = tb(es, "Rm", [128, 2, 4, 258]); bRm = Buf("Rm")
                    tot = tb(es, "tot", [128, 4, 258]); btot = Buf("tot")
                    rS = recvS.rearrange("(r p) c -> p r c", p=128)
                    ld("sp", bRm, Rm[:, 0], rS, reads=[brS], writes=[bRm])
                    ld("sp", bRm, Rm[0:64, 1], rS[64:128], reads=[brS], writes=[bRm])
                    ld("sp", bRm, Rm[64:128, 1], rS[0:64], reads=[brS], writes=[bRm])
                    kb.op("dve", V(nc.vector.tensor_tensor, out=tot[:], in0=Rm[:, 0], in1=Rm[:, 1], op=ALU.add), reads=[bRm], writes=[btot])
                    for hh in range(4):
                        diff_combine(tot[:, hh, 0:128], tot[:, hh, 129:257], tot[:, hh, 128:129], tot[:, hh, 257:258],
                                     oN[:, 16, hh * 128:(hh + 1) * 128], [btot], [boN[16]], l)
                    kb.barrier()

                with ExitStack() as es:
                    wB = tb(es, "wB", [128, 8, 1024], BF16); bwB = Buf("wB")
                    pw = tb(es, "pw", [128, 4, 128], BF16); bpw = Buf("pw")
                    wmat(wB[:], "w_out", l, bwB)
                    wload(pw[:], I["pool_w"][l].rearrange("g c d -> c g d"), bpw)
                    hsel_, bhu = halo_select(es, recvU, brU, 960, 15, "hu")
                    hsel = hsel_[:].rearrange("p (c t n) -> p c t n", t=16, n=15)
                    ext = tb(es, "ext", [128, 4, 4, 144]); bext = Buf("ext")
                    ptm = tb(es, "ptm", [128, 4, 128], BF16); bptm = Buf("ptm")
                    mixT = tb(es, "mixT", [128, 8, 128], BF16); bmix = Buf("mixT")
                    splg = tb(es, "splg", [128, 15, 128]); bspl = Buf("splg")
                    ptk = tb(es, "ptk", [128, 512]); bptk = Buf("ptk")
                    for t in range(17):
                        if t < 16:
                            kb.op("pool", V(nc.gpsimd.tensor_copy, out=ext[:, 0, :, 0:15], in_=hsel[:, :, t, :]), reads=[bhu], writes=[bext])
                            kb.op("pool", V(nc.gpsimd.tensor_copy, out=ext[:, 0, :, 15:143], in_=uT[:, :, t * 128:(t + 1) * 128]),
                                  reads=[buT[t]], writes=[bext])
                            kb.op("dve", V(nc.vector.tensor_tensor, out=ext[:, 1, :, 1:143], in0=ext[:, 0, :, 1:143], in1=ext[:, 0, :, 0:142], op=ALU.add),
                                  reads=[bext], writes=[bext])
                            kb.op("dve", V(nc.vector.tensor_tensor, out=ext[:, 2, :, 3:143], in0=ext[:, 1, :, 3:143], in1=ext[:, 1, :, 1:141], op=ALU.add),
                                  reads=[bext], writes=[bext])
                            kb.op("dve", V(nc.vector.tensor_tensor, out=ext[:, 3, :, 7:143], in0=ext[:, 2, :, 7:143], in1=ext[:, 2, :, 3:139], op=ALU.add),
                                  reads=[bext], writes=[bext])
                            kb.op("dve", V(nc.vector.tensor_tensor, out=ext[:, 1, 3, 15:143], in0=ext[:, 3, 3, 15:143], in1=ext[:, 3, 3, 7:135], op=ALU.add),
                                  reads=[bext], writes=[bext])
                            wsrc = [ext[:, 1, 0, 15:143], ext[:, 2, 1, 15:143], ext[:, 3, 2, 15:143], ext[:, 1, 3, 15:143]]
                            for g in range(4):
                                if t == 0:
                                    kb.op("dve", V(nc.vector.tensor_tensor, out=wsrc[g], in0=wsrc[g], in1=invc0[:, g, :], op=ALU.mult),
                                          reads=[bext, binv], writes=[bext])
                                    kb.op("dve", V(nc.vector.tensor_tensor, out=ptm[:, g, :], in0=wsrc[g], in1=ext[:, 0, g, 15:143], op=ALU.subtract),
                                          reads=[bext], writes=[bptm])
                                else:
                                    kb.op("dve", V(nc.vector.scalar_tensor_tensor, out=ptm[:, g, :], in0=wsrc[g], scalar=1.0 / (2 << g),
                                                   in1=ext[:, 0, g, 15:143], op0=ALU.mult, op1=ALU.subtract), reads=[bext], writes=[bptm])
                        else:
                            for g in range(4):
                                w = 2 << g
                                ld("sp", bspl, splg[:], I["spool"][l].rearrange("p (r c) -> p r c", c=512)[:, :, g * 128:(g + 1) * 128], writes=[bspl])
                                kb.op("dve", V(nc.vector.tensor_reduce, out=ptk[:, g * 128:(g + 1) * 128],
                                               in_=splg[:].rearrange("p r c -> p c r")[:, :, 15 - (w - 1):15], axis=AX.X, op=ALU.add),
                                      reads=[bspl], writes=[bptk])
                                kb.op("dve", V(nc.vector.tensor_tensor, out=ptk[:, g * 128:(g + 1) * 128], in0=ptk[:, g * 128:(g + 1) * 128],
                                               in1=zs[:, g * 128:(g + 1) * 128], op=ALU.add), reads=[bptk, bzs], writes=[bptk])
                                kb.op("dve", V(nc.vector.scalar_tensor_tensor, out=ptk[:, g * 128:(g + 1) * 128], in0=ptk[:, g * 128:(g + 1) * 128],
                                               scalar=1.0 / w, in1=zs[:, g * 128:(g + 1) * 128], op0=ALU.mult, op1=ALU.subtract),
                                      reads=[bptk, bzs], writes=[bptk])
                            ps, bp = nps()
                            for g in range(4):
                                tr(ps[:, g * 128:(g + 1) * 128], ptk[:, g * 128:(g + 1) * 128], ident_f[:], [bptk, bidf], [bp])
                            evac(ptm[:], ps[:].rearrange("p (g n) -> p g n", g=4), [bp], [bptm])
                        for g in range(4):
                            ps, bp = nps()
                            mm(ps[:, 0:128], pw[:, g, :], ptm[:, g, :], True, True, [bpw, bptm], [bp])
                            kb.op("dve", V(nc.vector.tensor_scalar, out=mixT[:, g, :], in0=ps[:, 0:128], scalar1=psc[:, l, g:g + 1], scalar2=None,
                                           op0=ALU.mult), reads=[bp, bpsc], writes=[bmix])
                        for hh in range(4):
                            tr(psb[:, hh * 128:(hh + 1) * 128], oN[:, t, hh * 128:(hh + 1) * 128], ident_b[:], [boN[t], bidb], [bpsb])
                        evac(mixT[:, 4:8, :], psb[:, 0:512].rearrange("p (g n) -> p g n", g=4), [bpsb], [bmix])
                        for dc in range(8):
                            ps, bp = nps()
                            for kc in range(8):
                                mm(ps[:, 0:128], wB[:, kc, dc * 128:(dc + 1) * 128], mixT[:, kc, :], kc == 0, kc == 7, [bwB, bmix], [bp])
                            kb.op("dve", V(nc.vector.tensor_tensor, out=xT[:, dc, t * 128:(t + 1) * 128], in0=xT[:, dc, t * 128:(t + 1) * 128],
                                           in1=ps[:, 0:128], op=ALU.add), reads=[bp, bx[t]], writes=[bx[t]])
                    kb.barrier()
            if STOP <= 4:
                break

            with ExitStack() as es:
                wQ = tb(es, "wQ", [128, 8, 1024], BF16); bwQ = Buf("wQ")
                wO = tb(es, "wO", [128, 8, 1024], BF16); bwO = Buf("wO")
                memKT = tb(es, "memKT", [128, 8, 256], BF16); bmk = Buf("memKT")
                memV1 = tb(es, "memV1", [128, 2, 4, 272], BF16); bmv = Buf("memV1")
                qcT = tb(es, "qcT", [128, 8, 128], BF16); bqc = Buf("qcT")
                PcT = tb(es, "PcT", [128, 2, 128], BF16); bPc = Buf("PcT")
                ocn = tb(es, "ocn", [128, 1024], BF16); bocn = Buf("ocn")
                wqh = tb(es, "wqh", [128, 8, 256], BF16); bwqh = Buf("wqh")
                qch = tb(es, "qch", [128, 256]); bqch = Buf("qch")
                ck(4.05)
                wmat(wQ[:], "wq_c", l, bwQ)
                wmat(wqh[:], "wq_ch", l, bwqh)
                ck(4.1)
                with ExitStack() as es2:
                    wKV = tb(es2, "wKV", [128, 8, 1024], BF16); bwKV = Buf("wKV")
                    for kv in range(2):
                        wmat(wKV[:], "wk_c" if kv == 0 else "wv_c", l, bwKV)
                        for mh in range(2):
                            st, bs = nstg()
                            for hf in range(2):
                                ps, bp = nps()
                                for kc in range(8):
                                    mm(ps[:], mpT[:, kc, mh * 128:(mh + 1) * 128], wKV[:, kc, hf * 512:(hf + 1) * 512], kc == 0, kc == 7, [bwKV, bmpT], [bp])
                                evac(st[:, hf * 512:(hf + 1) * 512], ps[:], [bp], [bs], eng="dve")
                                if kv == 1 and not os.environ.get("KNOMV"):
                                    for h2 in range(2):
                                        evac(memV1[:, mh, hf * 2 + h2, 0:256], ps[:, h2 * 256:(h2 + 1) * 256], [bp], [bmv], eng="dve")
                            ld("sp", bs, O["memk" if kv == 0 else "memv"][l, mh * 128:(mh + 1) * 128, :], st, reads=[bs], writes=[bout])
                        ck(4.15 + 0.1 * kv)
                        if kv == 0:
                            for dc in range(8):
                                ps, bp = nps()
                                for kc in range(8):
                                    mm(ps[:, 0:256], wKV[:, kc, dc * 128:(dc + 1) * 128], mpT[:, kc, :], kc == 0, kc == 7, [bwKV, bmpT], [bp])
                                evac(memKT[:, dc, :], ps[:, 0:256], [bp], [bmk])
                            ck(4.2)
                    for mh_ in range(2):
                        for hc_ in range(4):
                            kb.op("pool", V(nc.gpsimd.memset, memV1[:, mh_, hc_, 256:257], 1.0), writes=[bmv])
                    kb.barrier()
                ck(4.3)
                wmat(wO[:], "wo_c", l, bwO)
                for t in range(16):
                    norm_tile(t, gvec[:, l, 1, :])
                    for dc in range(8):
                        ps, bp = nps()
                        for kc in range(8):
                            mm(ps[:, 0:128], wQ[:, kc, dc * 128:(dc + 1) * 128], hTt[:, kc, :], kc == 0, kc == 7, [bwQ, bht], [bp])
                        evac(qcT[:, dc, :], ps[:, 0:128], [bp], [bqc])
                    for hc in range(4):
                        ps, bp = nps()
                        for mc in range(2):
                            for dch in range(2):
                                mm(ps[:, mc * 128:(mc + 1) * 128], memKT[:, hc * 2 + dch, mc * 128:(mc + 1) * 128], qcT[:, hc * 2 + dch, :],
                                   dch == 0, dch == 1, [bmk, bqc], [bp])
                        kb.op("act", V(nc.scalar.activation, out=PcT[:], in_=ps[:, 0:256].rearrange("p (m n) -> p m n", m=2), func=AF.Exp,
                                       scale=1.0 / 16.0), reads=[bp], writes=[bPc])
                        ps2, bp2 = nps()
                        for mc in range(2):
                            mm(ps2[:, 0:257], PcT[:, mc, :], memV1[:, mc, hc, 0:257], mc == 0, mc == 1, [bPc, bmv], [bp2])
                        kb.op("dve", V(nc.vector.reciprocal, out=small[:, 8:9], in_=ps2[:, 256:257]), reads=[bp2], writes=[bsm])
                        kb.op("dve", V(nc.vector.tensor_scalar, out=ocn[:, hc * 256:(hc + 1) * 256], in0=ps2[:, 0:256], scalar1=small[:, 8:9],
                                       scalar2=None, op0=ALU.mult), reads=[bp2, bsm], writes=[bocn])
                    for dc in range(8):
                        tr(psb[:, dc * 128:(dc + 1) * 128], ocn[:, dc * 128:(dc + 1) * 128], ident_b[:], [bocn, bidb], [bpsb])
                    evac(qcT[:], psb[:].rearrange("p (g n) -> p g n", g=8), [bpsb], [bqc])
                    for dc in range(8):
                        ps, bp = nps()
                        for kc in range(8):
                            mm(ps[:, 0:128], wO[:, kc, dc * 128:(dc + 1) * 128], qcT[:, kc, :], kc == 0, kc == 7, [bwO, bqc], [bp])
                        kb.op("dve", V(nc.vector.tensor_tensor, out=xT[:, dc, t * 128:(t + 1) * 128], in0=xT[:, dc, t * 128:(t + 1) * 128],
                                       in1=ps[:, 0:128], op=ALU.add), reads=[bp, bx[t]], writes=[bx[t]])
                ck(4.5)
                xtl = tb(es, "xtl", [128, 8, 16, 2]); bxtl = Buf("xtl")
                kb.op("dve", V(nc.vector.tensor_copy, out=xtl[:], in_=xT[:, :, 0:2048].rearrange("p c (t n) -> p c t n", n=128)[:, :, :, 126:128]),
                      reads=bx[0:16], writes=[bxtl])
                ld("sp", bxtl, sendX, xtl[:].rearrange("p c t n -> p (c t n)"), reads=[bxtl], writes=[bsX])
                allgather(sendX, recvX, bsX, brX, G4)
                ck(4.6)
                with ExitStack() as es2:
                    kq = tb(es2, "kq", [128, 2, 2048]); bkq = [Buf("kq0"), Buf("kq1")]
                    prod = tb(es2, "prod", [128, 2, 2048]); bprod = [Buf("prod0"), Buf("prod1")]
                    Sc = tb(es2, "Sc", [128, 128]); bS = Buf("Sc")
                    accs = tb(es2, "accs", [128, 257]); bacc = Buf("accs")
                    part = tb(es2, "part", [128, 256]); bpart = Buf("part")
                    Rm = tb(es2, "Rm", [128, 2, 4, 257]); bRm = Buf("Rm")
                    tot = tb(es2, "tot", [128, 4, 257]); btot = Buf("tot")
                    norm_tile(16, gvec[:, l, 1, :])
                    ps, bp = nps()
                    for kc in range(8):
                        mm(ps[:, 0:256], hTt[:, kc, :], wqh[:, kc, :], kc == 0, kc == 7, [bwqh, bht], [bp])
                    evac(qch[:], ps[:, 0:256], [bp], [bqch])
                    qc_b = BC(qch[:], 1, 8)
                    for pc in range(16):
                        i = pc % 2
                        ld("sp", bkq[i], kq[:, i, :], I["cmk"][l][:, pc * 2048:(pc + 1) * 2048], writes=[bkq[i]])
                        kb.op("pool", V(nc.gpsimd.tensor_tensor, out=prod[:, i, :].rearrange("p (n d) -> p n d", d=256),
                                        in0=kq[:, i, :].rearrange("p (n d) -> p n d", d=256), in1=qc_b, op=ALU.mult),
                              reads=[bkq[i], bqch], writes=[bprod[i]])
                        kb.op("dve", V(nc.vector.tensor_reduce, out=Sc[:, pc * 8:pc * 8 + 8], in_=prod[:, i, :].rearrange("p (n d) -> p n d", d=256),
                                       axis=AX.X, op=ALU.add), reads=[bprod[i]], writes=[bS])
                    kb.op("dve", V(nc.vector.memset, accs[:], 0.0), writes=[bacc])
                    kb.op("act", V(nc.scalar.activation, out=Sc[:], in_=Sc[:], func=AF.Exp, scale=1.0 / 16.0, accum_out=accs[:, 256:257]),
                          reads=[bS, bacc], writes=[bS, bacc])
                    for pc in range(16):
                        i = pc % 2
                        ld("sp", bkq[i], kq[:, i, :], I["cmv"][l][:, pc * 2048:(pc + 1) * 2048], writes=[bkq[i]])
                        kb.op("pool", V(nc.gpsimd.tensor_tensor, out=prod[:, i, :].rearrange("p (n d) -> p n d", d=256),
                                        in0=kq[:, i, :].rearrange("p (n d) -> p n d", d=256), in1=BC(Sc[:, pc * 8:pc * 8 + 8], 2, 256), op=ALU.mult),
                              reads=[bkq[i], bS], writes=[bprod[i]])
                        kb.op("dve", V(nc.vector.tensor_reduce, out=part[:], in_=prod[:, i, :].rearrange("p (n d) -> p d n", d=256),
                                       axis=AX.X, op=ALU.add), reads=[bprod[i]], writes=[bpart])
                        kb.op("dve", V(nc.vector.tensor_tensor, out=accs[:, 0:256], in0=accs[:, 0:256], in1=part[:], op=ALU.add),
                              reads=[bpart, bacc], writes=[bacc])
                    ld("sp", bacc, sendC, accs[:], reads=[bacc], writes=[bsC])
                    allgather(sendC, recvC, bsC, brC, G4)
                    rC = recvC.rearrange("(r p) c -> p r c", p=128)
                    ld("sp", bRm, Rm[:, 0], rC, reads=[brC], writes=[bRm])
                    ld("sp", bRm, Rm[0:64, 1], rC[64:128], reads=[brC], writes=[bRm])
                    ld("sp", bRm, Rm[64:128, 1], rC[0:64], reads=[brC], writes=[bRm])
                    kb.op("dve", V(nc.vector.tensor_tensor, out=tot[:], in0=Rm[:, 0], in1=Rm[:, 1], op=ALU.add), reads=[bRm], writes=[btot])
                    for hc in range(4):
                        kb.op("dve", V(nc.vector.reciprocal, out=small[:, 8:9], in_=tot[:, hc, 256:257]), reads=[btot], writes=[bsm])
                        kb.op("dve", V(nc.vector.tensor_scalar, out=ocn[:, hc * 256:(hc + 1) * 256], in0=tot[:, hc, 0:256], scalar1=small[:, 8:9],
                                       scalar2=None, op0=ALU.mult), reads=[btot, bsm], writes=[bocn])
                    for dc in range(8):
                        tr(psb[:, dc * 128:(dc + 1) * 128], ocn[:, dc * 128:(dc + 1) * 128], ident_b[:], [bocn, bidb], [bpsb])
                    evac(qcT[:], psb[:].rearrange("p (g n) -> p g n", g=8), [bpsb], [bqc])
                    for dc in range(8):
                        ps, bp = nps()
                        for kc in range(8):
                            mm(ps[:, 0:128], wO[:, kc, dc * 128:(dc + 1) * 128], qcT[:, kc, :], kc == 0, kc == 7, [bwO, bqc], [bp])
                        kb.op("dve", V(nc.vector.tensor_tensor, out=xT[:, dc, 2048:2176], in0=xT[:, dc, 2048:2176], in1=ps[:, 0:128], op=ALU.add),
                              reads=[bp, bx[16]], writes=[bx[16]])
                    kb.barrier()
                ck(4.8)
                hx_, bhx = halo_select(es, recvX, brX, 256, 2, "hx")
                kb.op("dve", V(nc.vector.tensor_copy, out=xT[:, :, HB:HB + 32], in_=hx_[:].rearrange("p (c n) -> p c n", c=8)), reads=[bhx], writes=[bxh])
                kb.barrier()
            if STOP <= 5:
                break

            with ExitStack() as es:
                hT = tb(es, "hT", [128, 8, 17 * 130], BF16); bh = [Buf("h%d" % t) for t in range(17)]
                hh = tb(es, "hh", [128, 8, 32], BF16); bhh = Buf("hh")
                wup = tb(es, "wup", [128, 4, 8, 256], BF16); bwup = [Buf("wup%d" % i) for i in range(4)]
                wdn = tb(es, "wdn", [128, 4, 1024], BF16); bwdn = [Buf("wdn%d" % i) for i in range(4)]
                scs = tb(es, "scs", [128, 4, 2, 2, 128]); bscs = [Buf("scs%d" % i) for i in range(4)]
                cva = tb(es, "cva", [128, 2, 3, 128]); bcva = Buf("cva")
                cvg = tb(es, "cvg", [128, 2, 3, 128]); bcvg = Buf("cvg")
                actT = tb(es, "actT", [128, 2, 384], BF16); bact = [Buf("act0"), Buf("act1")]
                ups = tb(es, "ups", [128, 4, 256]); bups = [Buf("ups%d" % i) for i in range(4)]
                cvst = tb(es, "cvst", [128, 44, 2]); bcvst = Buf("cvst")
                gf = gvec[:, l, 2, :]
                for t in range(17):
                    norm(t * 128, 128, gf, lambda kc, t=t: hT[:, kc, t * 130 + 2:t * 130 + 130], [bx[t]], [bh[t]])
                norm(HB, 32, gf, lambda kc: hh[:, kc, :], [bxh], [bhh])
                kb.op("dve", V(nc.vector.tensor_copy, out=hT[:, :, 0:2080].rearrange("p c (t n) -> p c t n", n=130)[:, :, :, 0:2],
                               in_=hh[:].rearrange("p c (t n) -> p c t n", n=2)), reads=[bhh], writes=bh[0:16])
                kb.op("pool", V(nc.gpsimd.memset, hT[:, :, 2080:2082], 0.0), writes=[bh[16]])
                groups = [(0, 3), (3, 3), (6, 3), (9, 3), (12, 3), (15, 1), (16, 1)]
                ai = 0
                for p2 in range(0, NPAIR, 2):
                    for q_ in range(2):
                        p = p2 + q_
                        wi = p % 4
                        wmat(wup[:, wi, :, 0:128], "w_up", l, bwup[wi], p * 128, (p + 1) * 128)
                        wmat(wup[:, wi, :, 128:256], "w_up", l, bwup[wi], DFF + p * 128, DFF + (p + 1) * 128)
                        wload(wdn[:, wi, :], I["w_down"][l][p * 128:(p + 1) * 128, :], bwdn[wi])
                        for ag in range(2):
                            ld("sp", bscs[wi], scs[:, wi, ag],
                               I["sconv"][l].rearrange("p (r f) -> p r f", r=2)[:, :, ag * DFF + p * 128:ag * DFF + (p + 1) * 128], writes=[bscs[wi]])
                    for (t0, nt) in groups:
                        for q_ in range(2):
                            p = p2 + q_
                            wi = p % 4
                            ncol = nt * 130
                            c0 = t0 * 130
                            pa, bpa = nps()
                            pg, bpg = nps()
                            for ag, (pp, bpp) in enumerate(((pa, bpa), (pg, bpg))):
                                for kc in range(8):
                                    mm(pp[:, 0:ncol], wup[:, wi, kc, ag * 128:(ag + 1) * 128], hT[:, kc, c0:c0 + ncol], kc == 0, kc == 7,
                                       [bwup[wi]] + bh[t0:t0 + nt], [bpp])
                            pst = None
                            if t0 == 16:
                                pst, bpst = nps()
                                for ag in range(2):
                                    for r in range(2):
                                        tr(pst[:, (ag * 2 + r) * 128:(ag * 2 + r + 1) * 128], scs[:, wi, ag, r, :], ident_f[:], [bscs[wi], bidf], [bpst])
                            for ag, (pp, bpp, cv, bcv) in enumerate(((pa, bpa, cva, bcva), (pg, bpg, cvg, bcvg))):
                                ch = ag * NPAIR + p
                                v3 = pp[:, 0:ncol].rearrange("p (t n) -> p t n", n=130)
                                if t0 == 16:
                                    in1 = pst[:, (ag * 2 + 1) * 128:(ag * 2 + 2) * 128].rearrange("p (t n) -> p t n", t=1)
                                    in0 = pst[:, (ag * 2) * 128:(ag * 2 + 1) * 128].rearrange("p (t n) -> p t n", t=1)
                                    rdx = [bpp, bpst]
                                else:
                                    in1 = v3[:, :, 1:129]
                                    in0 = v3[:, :, 0:128]
                                    rdx = [bpp]
                                kb.op("act", V(nc.scalar.activation, out=cv[:, 0, 0:nt, :], in_=v3[:, :, 2:130], func=AF.Identity,
                                               bias=cbv[:, ch:ch + 1], scale=cw[:, 2, ch:ch + 1]), reads=[bpp, bcw, bcb], writes=[bcv])
                                kb.op("dve", V(nc.vector.scalar_tensor_tensor, out=cv[:, 1, 0:nt, :], in0=in1, scalar=cw[:, 1, ch:ch + 1],
                                               in1=cv[:, 0, 0:nt, :], op0=ALU.mult, op1=ALU.add), reads=rdx + [bcv, bcw], writes=[bcv])
                                kb.op("dve", V(nc.vector.scalar_tensor_tensor, out=cv[:, 0, 0:nt, :], in0=in0, scalar=cw[:, 0, ch:ch + 1],
                                               in1=cv[:, 1, 0:nt, :], op0=ALU.mult, op1=ALU.add), reads=rdx + [bcv, bcw], writes=[bcv])
                                if t0 == 15:
                                    kb.op("dve", V(nc.vector.tensor_copy, out=cvst[:, ch, :], in_=v3[:, 0, 128:130]), reads=[bpp], writes=[bcvst])
                            kb.op("act", V(nc.scalar.activation, out=cvg[:, 1, 0:nt, :], in_=cvg[:, 0, 0:nt, :], func=AF.Silu), reads=[bcvg], writes=[bcvg])
                            a_i = q_
                            pass
                            kb.op("dve", V(nc.vector.tensor_tensor, out=actT[:, a_i, 0:nt * 128].rearrange("p (t n) -> p t n", n=128),
                                           in0=cva[:, 0, 0:nt, :], in1=cvg[:, 1, 0:nt, :], op=ALU.mult), reads=[bcva, bcvg], writes=[bact[a_i]])
                            if t0 == 16:
                                ps, bp = nps()
                                for kc in range(8):
                                    mm(ps[:, 0:256], hT[:, kc, 16 * 130 + 2:17 * 130], wup[:, wi, kc, :], kc == 0, kc == 7, [bwup[wi], bh[16]], [bp])
                                evac(ups[:, wi, :], ps[:, 0:256], [bp], [bups[wi]])
                                for ag in range(2):
                                    ld("sp", bups[wi], O["conv_s"][l][:, 5632 + ag * DFF + p * 128:5632 + ag * DFF + (p + 1) * 128],
                                       ups[:, wi, ag * 128:(ag + 1) * 128], reads=[bups[wi]], writes=[bout])
                        for dc in range(8):
                            ps, bp = nps()
                            for q_ in range(2):
                                wi = (p2 + q_) % 4
                                mm(ps[:, 0:nt * 128], wdn[:, wi, dc * 128:(dc + 1) * 128], actT[:, q_, 0:nt * 128], q_ == 0, q_ == 1, [bwdn[wi], bact[q_]], [bp])
                            kb.op("dve", V(nc.vector.tensor_tensor, out=xT[:, dc, t0 * 128:(t0 + nt) * 128], in0=xT[:, dc, t0 * 128:(t0 + nt) * 128],
                                           in1=ps[:, 0:nt * 128], op=ALU.add), reads=[bp] + bx[t0:t0 + nt], writes=bx[t0:t0 + nt])
                for t_ in range(2):
                    ld("sp", bcvst, O["conv_p"][l][t_].rearrange("(ch p) -> p ch", p=128), cvst[:, :, t_], reads=[bcvst], writes=[bout], slow=True)
                ld("sp", bout, O["conv_s"][l][:, 0:5632], I["sconv"][l][:, 5632:11264], writes=[bout])
                kb.barrier()

    except _Stop:
        pass

    xn = sb("xn", [128, 8, 128]); bxn = Buf("xn")
    for t in range(17):
        norm(t * 128, 128, gfin, lambda kc: xn[:, kc, :], [bx[t]], [bxn])
        st, bs = nstg()
        for half in range(2):
            ps, bp = nps()
            for k4 in range(4):
                kc = half * 4 + k4
                tr(ps[:, k4 * 128:(k4 + 1) * 128], xn[:, kc, :], ident_f[:], [bxn, bidf], [bp])
            evac(st[:, half * 512:(half + 1) * 512], ps[:], [bp], [bs])
        dst = O["y_p"][t * 128:(t + 1) * 128, :] if t < 16 else O["y_s"]
        ld("sp", bs, dst, st, reads=[bs], writes=[bout])
    kb.barrier()
    return nc, kb


_PROG = {}


def _slopes():
    return 2.0 ** (-8.0 * np.arange(1, 5) / 4.0)


def kernel(**inp):
    inp = {k: np.asarray(v) for k, v in inp.items()}
    if "nc" not in _PROG:
        _PROG["nc"], _PROG["kb"] = build_program()
    nc = _PROG["nc"]
    f32 = np.float32
    slopes = _slopes()
    in_maps = []
    xp = inp["x_prompt"]
    pid = np.arange(128)
    for c in range(8):
        b, j = c // 4, c % 4
        h, g = c % 4, c // 4
        sq_ = np.concatenate([np.arange(64 * g, 64 * g + 64)] * 2)
        hf_ = np.repeat(np.arange(2), 64)
        m = {}
        m["xp"] = np.ascontiguousarray(xp[b].reshape(16, 4, 128, 1024)[:, j].reshape(2048, 1024))
        m["xs"] = np.ascontiguousarray(inp["x_sample"].reshape(128, 1024)[sq_])
        m["memp"] = np.ascontiguousarray(inp["mem_prompt"][b])
        for l_ in range(2):
            m["ck%d" % l_] = np.ascontiguousarray(inp["cache_k"][l_, :, :, h, :]).reshape(20480, 2048)
            m["cv%d" % l_] = np.ascontiguousarray(inp["cache_v"][l_, :, :, h, :]).reshape(20480, 2048)
        cmk_ = inp["cache_mem_k"][:, 64 * g:64 * g + 64, :, h, :].reshape(2, 64, 2, 128 * 256)
        cmv_ = inp["cache_mem_v"][:, 64 * g:64 * g + 64, :, h, :].reshape(2, 64, 2, 128 * 256)
        m["cmk"] = np.ascontiguousarray(cmk_.transpose(0, 2, 1, 3).reshape(2, 128, 32768))
        m["cmv"] = np.ascontiguousarray(cmv_.transpose(0, 2, 1, 3).reshape(2, 128, 32768))
        m["spool"] = np.ascontiguousarray(inp["state_pool"].reshape(2, 128, 7680)[:, sq_])
        m["sconv"] = np.ascontiguousarray(inp["state_conv"].reshape(2, 128, 11264)[:, sq_])
        pt_ = inp["page_table"][64 * g:64 * g + 64].reshape(64, 2, 8)
        m["ptab"] = np.ascontiguousarray(pt_.transpose(1, 0, 2).reshape(128, 8)).astype(np.int32)
        for k in ("g_mix", "w_in", "pool_w", "pool_scale", "subln_g", "w_out", "g_cross", "wq_c", "wk_c", "wv_c", "wo_c",
                  "g_ffn", "w_up", "conv_w", "conv_b", "w_down", "g_final"):
            m[k] = inp[k]
        w_in = inp["w_in"]
        m["w_in_h"] = np.ascontiguousarray(np.concatenate(
            [w_in[:, :, 512 + h * 128:512 + (h + 1) * 128], w_in[:, :, 1024 + h * 128:1024 + (h + 1) * 128],
             w_in[:, :, 1536 + h * 128:1536 + (h + 1) * 128]], axis=2))
        m["wq_ch"] = np.ascontiguousarray(inp["wq_c"][:, :, h * 256:(h + 1) * 256])
        m["lam4"] = np.ascontiguousarray(np.stack([inp["lam_q1"], inp["lam_k1"], inp["lam_q2"], inp["lam_k2"]], axis=1))
        m["ident"] = np.eye(128, dtype=f32)
        sel = np.zeros((128, 4), f32)
        sel[:, (j - 1) if j >= 1 else 3] = 1.0
        m["sel"] = sel
        md = np.zeros((128, 4, 128), f32)
        for e in range(4):
            if e > 3 - j:
                md[:, e, :] = 1.0
            elif e == 3 - j:
                md[:, e, :] = (pid[:, None] <= pid[None, :]).astype(f32)
        m["maskd"] = md
        ap_ = np.zeros((128, 4, 64), np.float64)
        for hh in range(4):
            for e in range(64):
                dd = e + j - 3
                ap_[:, hh, e] = slopes[hh] * (pid - 128.0 * dd - 64.0) if dd >= 0 else -30000.0
        m["alibi_p"] = ap_.reshape(128, 256).astype(f32)
        kpos = 1024 * hf_[:, None] + np.arange(1024)[None, :]
        als = np.zeros((128, 1025), np.float64)
        als[:, :1024] = -slopes[h] * (2048.0 - kpos)
        m["alibi_s"] = als.astype(f32)
        m["isnew"] = hf_.astype(f32).reshape(128, 1)
        ic = np.zeros((128, 4, 128), f32)
        for g in range(4):
            w = 2 << g
            ic[:, g, :] = 1.0 / w
            if j == 0:
                ic[:, g, :] = 1.0 / np.minimum(np.arange(128) + 1, w)[None, :]
        m["invc0"] = ic
        wt = np.ones((128, 4, 4), np.float64)
        for hh in range(1, 4):
            for i4 in range(4):
                wt[:, hh, i4] = np.exp(slopes[hh] * (128.0 * i4 + pid - 64.0))
        m["wtab"] = wt.reshape(128, 16).astype(f32)
        in_maps.append({k: np.ascontiguousarray(v) for k, v in m.items()})
    res = run_bass_kernel_spmd(nc, in_maps, core_ids=list(range(8)))
    R = res.results
    y_p = np.zeros((2, 64, 128, 1024), f32)
    k_p = np.zeros((2, 2, 64, 128, 512), f32)
    v_p = np.zeros((2, 2, 64, 128, 512), f32)
    for c in range(8):
        b, j = c // 4, c % 4
        y_p[b, j::4] = R[c]["y_p"].reshape(16, 128, 1024)
        k_p[:, b, j::4] = R[c]["k_p"].reshape(2, 16, 128, 512)
        v_p[:, b, j::4] = R[c]["v_p"].reshape(2, 16, 128, 512)
    y_p = y_p.reshape(2, 8192, 1024)
    k_p = k_p.reshape(2, 2, 8192, 4, 128)
    v_p = v_p.reshape(2, 2, 8192, 4, 128)
    def samp(name):
        return np.concatenate([R[0][name][..., 0:64, :], R[4][name][..., 0:64, :]], axis=-2)
    y_s = samp("y_s").reshape(128, 1, 1024)
    memk = np.stack([R[0]["memk"], R[4]["memk"]], axis=1).reshape(2, 2, 256, 4, 256)
    memv = np.stack([R[0]["memv"], R[4]["memv"]], axis=1).reshape(2, 2, 256, 4, 256)
    pool_p = np.stack([R[3]["pool_p"], R[7]["pool_p"]], axis=1)
    conv_p = np.stack([R[3]["conv_p"], R[7]["conv_p"]], axis=1)
    k_s = samp("k_s").reshape(2, 128, 1, 4, 128)
    v_s = samp("v_s").reshape(2, 128, 1, 4, 128)
    pool_s = samp("pool_s").reshape(2, 128, 15, 512)
    conv_s = samp("conv_s").reshape(2, 128, 2, 5632)
    return (y_p, y_s, k_p, v_p, memk, memv, pool_p, conv_p, k_s, v_s, pool_s, conv_s)
```

```python
import math
import os
from contextlib import ExitStack
import numpy as np
import ml_dtypes
import concourse.bass as bass
import concourse.mybir as mybir
from concourse.bass_utils import run_bass_kernel_spmd

F32 = mybir.dt.float32
BF16 = mybir.dt.bfloat16
I32 = mybir.dt.int32
ALU = mybir.AluOpType
AF = mybir.ActivationFunctionType
AX = mybir.AxisListType

EPOCH = 30000
DEPTH = 2
NT = 16
TOK = 2176
HB = 2176
XC = 2208
DFF = 2816
NPAIR = 22
NROWS = 512 if os.environ.get("KSMALL") else 20480
NOAG = bool(os.environ.get("KNOAG"))


class Buf:
    __slots__ = ("name", "w", "r", "dsem", "dcnt")

    def __init__(self, name):
        self.name = name
        self.w = {}
        self.r = {}
        self.dsem = None
        self.dcnt = 0


def _merge(d, tok):
    k = id(tok[0])
    if k not in d or d[k][1] < tok[1]:
        d[k] = tok


class KB:
    def __init__(self, nc):
        self.nc = nc
        self.eng = {"pe": nc.tensor, "act": nc.scalar, "dve": nc.vector,
                    "pool": nc.gpsimd, "sp": nc.sync}
        self.esem = {e: nc.alloc_semaphore(name="es_%s_0" % e) for e in self.eng}
        self.ecnt = {e: 0 for e in self.eng}
        self.eep = {e: 0 for e in self.eng}
        self.seen = {e: {} for e in self.eng}
        self.dbufs = []
        self.dsems = {}
        self.dpool = []
        self.ninst = 0

    def _wait(self, e, tok):
        sem, val = tok
        k = id(sem)
        if self.seen[e].get(k, 0) >= val:
            return
        self.eng[e].wait_ge(sem, val)
        self.seen[e][k] = val

    def _deps(self, e, reads, writes):
        own = id(self.esem[e]) if e == "pe" else None
        for b in reads:
            for k, tok in b.w.items():
                if k != own:
                    self._wait(e, tok)
        for b in writes:
            for k, tok in b.w.items():
                if k != own:
                    self._wait(e, tok)
            for k, tok in b.r.items():
                if k != own:
                    self._wait(e, tok)

    def _mark(self, tok, reads, writes):
        for b in reads:
            _merge(b.r, tok)
        for b in writes:
            _merge(b.w, tok)
            b.r = {}

    def op(self, e, inst_fn, reads=(), writes=()):
        self._deps(e, reads, writes)
        if self.ecnt[e] >= EPOCH:
            self.eep[e] += 1
            self.esem[e] = self.nc.alloc_semaphore(name="es_%s_%d" % (e, self.eep[e]))
            self.ecnt[e] = 0
        inst = inst_fn()
        self.ecnt[e] += 1
        inst.then_inc(self.esem[e], 1)
        self._mark((self.esem[e], self.ecnt[e]), reads, writes)
        self.ninst += 1
        return inst

    def dma(self, q, dbuf, inst_fn, reads=(), writes=(), inc=16):
        self._deps(q, reads, writes)
        ent = self.dsems.get(dbuf.name)
        if ent is None and inc == 1:
            ent = [self.nc.alloc_semaphore(name="cs_" + dbuf.name), 0]
            self.dsems[dbuf.name] = ent
        if ent is None:
            if len(self.dpool) < 28:
                self.dpool.append([self.nc.alloc_semaphore(name="ds_%d" % len(self.dpool)), 0])
                ent = self.dpool[-1]
            else:
                ent = self.dpool[len(self.dsems) % 28]
            self.dsems[dbuf.name] = ent
        inst = inst_fn()
        ent[1] += inc
        inst.then_inc(ent[0], inc)
        self._mark((ent[0], ent[1]), reads, writes)
        self.ninst += 1
        return inst

    def barrier(self):
        toks = [(self.esem[f], self.ecnt[f]) for f in self.eng if self.ecnt[f] > 0]
        for ent in self.dpool:
            toks.append((ent[0], ent[1]))
        for e in self.eng:
            for t in toks:
                if t[0] is self.esem[e]:
                    continue
                self._wait(e, t)

    def wait_all(self, e, bufs):
        for b in bufs:
            for tok in list(b.w.values()) + list(b.r.values()):
                self._wait(e, tok)


def BC(ap, pos, n):
    lst = [list(x) for x in ap.ap]
    lst.insert(pos, [0, n])
    return bass.AP(ap.tensor, ap.offset, lst)


def PB(ap, n=128):
    lst = [[0, n]] + [list(x) for x in ap.ap]
    return bass.AP(ap.tensor, ap.offset, lst)


IN_SPECS = [
    ("xp", (2048, 1024), F32), ("xs", (128, 1024), F32), ("memp", (256, 1024), F32),
    ("ck0", (NROWS, 2048), F32), ("ck1", (NROWS, 2048), F32), ("cv0", (NROWS, 2048), F32), ("cv1", (NROWS, 2048), F32),
    ("cmk", (DEPTH, 128, 32768), F32), ("cmv", (DEPTH, 128, 32768), F32),
    ("spool", (DEPTH, 128, 7680), F32), ("sconv", (DEPTH, 128, 11264), F32),
    ("ptab", (128, 8), I32),
    ("g_mix", (DEPTH, 1024), F32), ("w_in", (DEPTH, 1024, 2048), F32), ("w_in_h", (DEPTH, 1024, 384), F32),
    ("pool_w", (DEPTH, 4, 128, 128), F32), ("pool_scale", (DEPTH, 512), F32),
    ("lam4", (DEPTH, 4, 64), F32), ("subln_g", (DEPTH, 128), F32),
    ("w_out", (DEPTH, 1024, 1024), F32), ("g_cross", (DEPTH, 1024), F32),
    ("wq_c", (DEPTH, 1024, 1024), F32), ("wq_ch", (DEPTH, 1024, 256), F32),
    ("wk_c", (DEPTH, 1024, 1024), F32), ("wv_c", (DEPTH, 1024, 1024), F32), ("wo_c", (DEPTH, 1024, 1024), F32),
    ("g_ffn", (DEPTH, 1024), F32), ("w_up", (DEPTH, 1024, 5632), F32),
    ("conv_w", (DEPTH, 3, 5632), F32), ("conv_b", (DEPTH, 5632), F32),
    ("w_down", (DEPTH, 2816, 1024), F32), ("g_final", (1024,), F32),
    ("ident", (128, 128), F32), ("sel", (128, 4), F32), ("maskd", (128, 4, 128), F32),
    ("alibi_p", (128, 256), F32), ("alibi_s", (128, 1025), F32), ("isnew", (128, 1), F32),
    ("invc0", (128, 4, 128), F32), ("wtab", (128, 16), F32),
]
OUT_SPECS = [
    ("y_p", (2048, 1024)), ("y_s", (128, 1024)), ("k_p", (DEPTH, 2048, 512)), ("v_p", (DEPTH, 2048, 512)),
    ("memk", (DEPTH, 256, 1024)), ("memv", (DEPTH, 256, 1024)), ("pool_p", (DEPTH, 15, 512)),
    ("conv_p", (DEPTH, 2, 5632)), ("k_s", (DEPTH, 128, 512)), ("v_s", (DEPTH, 128, 512)),
    ("pool_s", (DEPTH, 128, 7680)), ("conv_s", (DEPTH, 128, 11264)),
]


def build_program():
    nc = bass.Bass("TRN2", target_bir_lowering=False, num_devices=8)
    kb = KB(nc)
    STOP = float(os.environ.get("KSTOP", "99"))

    class _Stop(Exception):
        pass

    def ck(x):
        if STOP <= x:
            raise _Stop()
    I = {n: nc.dram_tensor(n, list(s), d, kind="ExternalInput").ap() for n, s, d in IN_SPECS}
    O = {n: nc.dram_tensor(n, list(s), F32, kind="ExternalOutput").ap() for n, s in OUT_SPECS}
    bout = Buf("out")

    def dint(name, shape, dt):
        return nc.dram_tensor(name, list(shape), dt, kind="Internal").ap()

    def sb(name, shape, dt=F32):
        return nc.alloc_sbuf_tensor("s_" + name, list(shape), dt)

    uid = {"i": 0}

    def tb(es, name, shape, dt=F32):
        uid["i"] += 1
        return es.enter_context(nc.sbuf_tensor("t_%s_%d" % (name, uid["i"]), list(shape), dt))

    G4 = [[0, 1, 2, 3], [4, 5, 6, 7]]
    G8 = [list(range(8))]

    psl = [nc.alloc_psum_tensor("ps%d" % i, [128, 512], F32) for i in range(7)]
    psb = nc.alloc_psum_tensor("psb", [128, 1024], BF16)
    bps = [Buf("ps%d" % i) for i in range(7)]
    bpsb = Buf("psb")
    rr = {"ps": 0, "ev": 0, "n": 7}

    def nps():
        i = rr["ps"] % rr["n"]
        rr["ps"] = i + 1
        return psl[i], bps[i]

    def evac(out_ap, in_ap, reads, writes, eng=None):
        if eng is None:
            eng = ("act", "dve")[rr["ev"] % 2]
            rr["ev"] += 1
        if eng == "act":
            kb.op("act", lambda: nc.scalar.activation(out=out_ap, in_=in_ap, func=AF.Copy), reads, writes)
        else:
            kb.op(eng, lambda: kb.eng[eng].tensor_copy(out=out_ap, in_=in_ap), reads, writes)

    def mm(ps_ap, lhsT, rhs, start, stop, reads, writes):
        kb.op("pe", lambda: nc.tensor.matmul(ps_ap, lhsT=lhsT, rhs=rhs, start=start, stop=stop), reads, writes)

    def tr(ps_ap, in_ap, idt, reads, writes):
        kb.op("pe", lambda: nc.tensor.transpose(ps_ap, in_ap, idt), reads, writes)

    def ld(q, dbuf, out_ap, in_ap, reads=(), writes=(), slow=False):
        e = kb.eng[q]
        if slow:
            kb.dma(q, dbuf, lambda: e.dma_start(out=out_ap, in_=in_ap, allow_slow_non_contiguous=True), reads, writes)
        else:
            kb.dma(q, dbuf, lambda: e.dma_start(out=out_ap, in_=in_ap), reads, writes)

    def V(fn, *a, **k):
        return lambda: fn(*a, **k)

    xT = sb("xT", [128, 8, XC])
    bx = [Buf("x%d" % t) for t in range(17)]
    bxh = Buf("xhalo")
    hTt = sb("hTt", [128, 2, 8, 128], BF16); bhts = [Buf("hTt0"), Buf("hTt1")]; hs = {"i": 0}
    ident_f = sb("ident_f", [128, 128]); bidf = Buf("idf")
    ident_b = sb("ident_b", [128, 128], BF16); bidb = Buf("idb")
    ones_f = sb("ones_f", [128, 128]); bones = Buf("ones")
    cst = sb("cst", [128, 4]); bcst = Buf("cst")
    gvec = sb("gvec", [128, DEPTH, 3, 8]); bgv = Buf("gvec")
    gfin = sb("gfin", [128, 8]); bgf = Buf("gfin")
    psc = sb("psc", [128, DEPTH, 4]); bpsc = Buf("psc")
    cw = sb("cw", [128, 3, 44]); bcw = Buf("cw")
    cbv = sb("cbv", [128, 44]); bcb = Buf("cb")
    gsub = sb("gsub", [128, 128]); bgsub = Buf("gsub")
    lamv = sb("lamv", [128, 4, 64]); blamv = Buf("lamv")
    lamt = sb("lamt", [128, 8]); blam = Buf("lam")
    selt = sb("selt", [128, 4]); bsel = Buf("sel")
    maskt = sb("maskt", [128, 4, 128], BF16); bmask = Buf("mask")
    alip = sb("alip", [128, 256]); balip = Buf("alip")
    isnew = sb("isnew", [128, 1]); bisn = Buf("isnew")
    invc0 = sb("invc0", [128, 4, 128]); binv = Buf("invc0")
    ptab = sb("ptab", [128, 8], I32); bptab = Buf("ptab")
    mpT = sb("mpT", [128, 8, 256], BF16); bmpT = Buf("mpT")
    stg = sb("stg", [128, 2, 1024]); bstg = [Buf("stg0"), Buf("stg1")]
    sq = sb("sq", [128, 8, 128]); bsq = Buf("sq")
    rstd = sb("rstd", [128, 128]); brs = Buf("rstd")
    small = sb("small", [128, 16]); bsm = Buf("small")
    otmp = sb("otmp", [128, 2, 128]); bot = Buf("otmp")
    stgc = {"i": 0}

    def nstg():
        i = stgc["i"]
        stgc["i"] = 1 - i
        return stg[:, i, :], bstg[i]

    ld("sp", bidf, ident_f[:], I["ident"], writes=[bidf])
    ld("pool", bidb, ident_b[:], I["ident"], writes=[bidb])
    kb.op("pool", V(nc.gpsimd.memset, ones_f[:], 1.0), writes=[bones])
    kb.op("pool", V(nc.gpsimd.memset, cst[:, 0:1], 1e-6), writes=[bcst])
    kb.op("pool", V(nc.gpsimd.memset, cst[:, 1:2], 1e-5), writes=[bcst])
    kb.op("pool", V(nc.gpsimd.memset, cst[:, 2:3], 0.0), writes=[bcst])
    for l in range(DEPTH):
        for i, nm in enumerate(("g_mix", "g_cross", "g_ffn")):
            ld("sp", bgv, gvec[:, l, i, :], I[nm][l].rearrange("(kc p) -> p kc", p=128), writes=[bgv], slow=True)
        ld("sp", bpsc, psc[:, l, :], I["pool_scale"][l].rearrange("(kc p) -> p kc", p=128), writes=[bpsc], slow=True)
    ld("sp", bgf, gfin[:], I["g_final"].rearrange("(kc p) -> p kc", p=128), writes=[bgf], slow=True)
    ld("sp", bsel, selt[:], I["sel"], writes=[bsel])
    ld("pool", bmask, maskt[:], I["maskd"], writes=[bmask])
    ld("sp", balip, alip[:], I["alibi_p"], writes=[balip])
    ld("sp", bisn, isnew[:], I["isnew"], writes=[bisn])
    ld("sp", binv, invc0[:], I["invc0"], writes=[binv])
    wtab = sb("wtab", [128, 16]); bwt = Buf("wtab")
    ld("sp", bwt, wtab[:], I["wtab"], writes=[bwt])
    ld("sp", bptab, ptab[:], I["ptab"], writes=[bptab])
    idxs = sb("idxs", [128, 8, 8], I32); bidx = Buf("idxs")
    for qt in range(8):
        kb.op("dve", V(nc.vector.tensor_scalar, out=idxs[:, :, qt], in0=ptab[:, :], scalar1=8, scalar2=qt, op0=ALU.mult, op1=ALU.add),
              reads=[bptab], writes=[bidx])

    def norm(c0, n, gap, dst_fn, xbufs, dbufs, epscol=0):
        kb.op("act", V(nc.scalar.activation, out=sq[:, :, 0:n], in_=xT[:, :, c0:c0 + n], func=AF.Square),
              reads=xbufs, writes=[bsq])
        ps, bp = nps()
        for kc in range(8):
            mm(ps[:, 0:n], ones_f[:], sq[:, kc, 0:n], kc == 0, kc == 7, [bones, bsq], [bp])
        kb.op("act", V(nc.scalar.activation, out=rstd[:, 0:n], in_=ps[:, 0:n], func=AF.Sqrt,
                       bias=cst[:, epscol:epscol + 1], scale=1.0 / 1024.0), reads=[bp, bcst], writes=[brs])
        kb.op("dve", V(nc.vector.reciprocal, out=rstd[:, 0:n], in_=rstd[:, 0:n]), reads=[brs], writes=[brs])
        for kc in range(8):
            kb.op("dve", V(nc.vector.scalar_tensor_tensor, out=dst_fn(kc), in0=xT[:, kc, c0:c0 + n], scalar=gap[:, kc:kc + 1],
                           in1=rstd[:, 0:n], op0=ALU.mult, op1=ALU.mult), reads=list(xbufs) + [brs, bgv, bgf], writes=dbufs)

    def norm_tile(t, gap):
        hs["i"] = t % 2
        norm(t * 128, 128, gap, lambda kc: hTt[:, hs["i"], kc, :], [bx[t]], [bhts[hs["i"]]])

    def transpose_in(src_ap, dst_fn, bdst):
        st, bs = nstg()
        ld("sp", bs, st, src_ap, writes=[bs])
        for half in range(2):
            ps, bp = nps()
            for k4 in range(4):
                kc = half * 4 + k4
                tr(ps[:, k4 * 128:(k4 + 1) * 128], st[:, kc * 128:(kc + 1) * 128], ident_f[:], [bs, bidf], [bp])
            evac(dst_fn(half), ps[:].rearrange("p (k n) -> p k n", k=4), [bp], [bdst])

    for t in range(17):
        src = I["xp"][t * 128:(t + 1) * 128, :] if t < 16 else I["xs"]
        transpose_in(src, lambda half, t=t: xT[:, half * 4:half * 4 + 4, t * 128:(t + 1) * 128], bx[t])
    for mh in range(2):
        transpose_in(I["memp"][mh * 128:(mh + 1) * 128, :],
                     lambda half, mh=mh: mpT[:, half * 4:half * 4 + 4, mh * 128:(mh + 1) * 128], bmpT)

    sendK = [dint("sendK%d" % i, [256, 2048], BF16) for i in range(2)]
    recvK = [dint("recvK%d" % i, [1024, 2048], BF16) for i in range(2)]
    sendV = [dint("sendV%d" % i, [1024, 512], BF16) for i in range(2)]
    recvV = [dint("recvV%d" % i, [4096, 512], BF16) for i in range(2)]
    sendU = dint("sendU", [128, 960], F32); recvU = dint("recvU", [512, 960], F32)
    sendX = dint("sendX", [128, 256], F32); recvX = dint("recvX", [512, 256], F32)
    sendS = dint("sendS", [128, 258], F32); recvS = dint("recvS", [512, 258], F32)
    sendC = dint("sendC", [128, 257], F32); recvC = dint("recvC", [512, 257], F32)
    bsU, brU, bsX, brX, bsS, brS, bsC, brC = [Buf("cc%d" % i) for i in range(8)]
    bsK = [Buf("sK0"), Buf("sK1")]; brK = [Buf("rK0"), Buf("rK1")]
    bsV = [Buf("sV0"), Buf("sV1")]; brV = [Buf("rV0"), Buf("rV1")]

    def allgather(send, recv, bs_, br_, groups):
        if NOAG:
            return
        kb.dma("pool", br_, V(nc.gpsimd.collective_compute, "AllGather", ALU.bypass, replica_groups=groups,
                              ins=[send], outs=[recv]), reads=[bs_], writes=[br_], inc=1)

    def wload(dst_ap, src_ap, wbuf):
        ld("pool", wbuf, dst_ap, src_ap, writes=[wbuf])

    def wmat(dst, name, l, wbuf, c0=0, c1=None):
        src = I[name][l]
        if c1 is not None:
            src = src[:, c0:c1]
        wload(dst, src.rearrange("(kc p) f -> p kc f", p=128), wbuf)

    def diff_combine(o1, o2, l1, l2, dst, rd, wr, l):
        kb.op("dve", V(nc.vector.reciprocal, out=small[:, 0:1], in_=l1), reads=rd, writes=[bsm])
        kb.op("dve", V(nc.vector.reciprocal, out=small[:, 1:2], in_=l2), reads=list(rd) + [bsm], writes=[bsm])
        kb.op("dve", V(nc.vector.tensor_tensor, out=small[:, 1:2], in0=small[:, 1:2], in1=lamt[:, 4 + l:5 + l], op=ALU.mult),
              reads=[bsm, blam], writes=[bsm])
        kb.op("dve", V(nc.vector.tensor_scalar, out=otmp[:, 0, :], in0=o1, scalar1=small[:, 0:1], scalar2=None, op0=ALU.mult),
              reads=list(rd) + [bsm], writes=[bot])
        kb.op("dve", V(nc.vector.scalar_tensor_tensor, out=otmp[:, 0, :], in0=o2, scalar=small[:, 1:2], in1=otmp[:, 0, :],
                       op0=ALU.mult, op1=ALU.add), reads=list(rd) + [bsm, bot], writes=[bot])
        kb.op("dve", V(nc.vector.memset, small[:, 2:3], 0.0), writes=[bsm])
        kb.op("act", V(nc.scalar.activation, out=otmp[:, 1, :], in_=otmp[:, 0, :], func=AF.Square, accum_out=small[:, 2:3]),
              reads=[bot, bsm], writes=[bot, bsm])
        kb.op("act", V(nc.scalar.activation, out=small[:, 3:4], in_=small[:, 2:3], func=AF.Sqrt, bias=cst[:, 1:2], scale=1.0 / 128.0),
              reads=[bsm, bcst], writes=[bsm])
        kb.op("dve", V(nc.vector.reciprocal, out=small[:, 3:4], in_=small[:, 3:4]), reads=[bsm], writes=[bsm])
        kb.op("dve", V(nc.vector.scalar_tensor_tensor, out=dst, in0=otmp[:, 0, :], scalar=small[:, 3:4], in1=gsub[:],
                       op0=ALU.mult, op1=ALU.mult), reads=[bot, bsm, bgsub], writes=wr)

    def halo_select(es, recv, brecv, ncols, inner, name):
        acc = tb(es, name + "a", [128, ncols]); tmp = tb(es, name + "t", [128, ncols])
        bacc_, btmp = Buf(name + "a"), Buf(name + "t")
        kb.op("dve", V(nc.vector.memset, acc[:], 0.0), writes=[bacc_])
        for r in range(4):
            ld("sp", btmp, tmp[:], recv[r * 128:(r + 1) * 128, :], reads=[brecv], writes=[btmp])
            if r < 3:
                kb.op("dve", V(nc.vector.scalar_tensor_tensor, out=acc[:], in0=tmp[:], scalar=selt[:, r:r + 1], in1=acc[:],
                               op0=ALU.mult, op1=ALU.add), reads=[btmp, bsel, bacc_], writes=[bacc_])
            else:
                a4 = acc[:].rearrange("p (c t n) -> p c t n", t=16, n=inner)
                t4 = tmp[:].rearrange("p (c t n) -> p c t n", t=16, n=inner)
                nch = ncols // (16 * inner)
                for c_ in range(nch):
                    kb.op("dve", V(nc.vector.scalar_tensor_tensor, out=a4[:, c_, 1:16, :], in0=t4[:, c_, 0:15, :], scalar=selt[:, 3:4],
                                   in1=a4[:, c_, 1:16, :], op0=ALU.mult, op1=ALU.add), reads=[btmp, bsel, bacc_], writes=[bacc_])
        return acc, bacc_

    try:
        for l in range(DEPTH):
            if STOP <= 0:
                break
            lam0 = 0.8 - 0.6 * math.exp(-0.3 * l)
            ld("sp", blamv, lamv[:], PB(I["lam4"][l]), writes=[blamv])
            kb.op("dve", V(nc.vector.tensor_tensor, out=lamv[:, 0, :], in0=lamv[:, 0, :], in1=lamv[:, 1, :], op=ALU.mult), reads=[blamv], writes=[blamv])
            kb.op("dve", V(nc.vector.tensor_tensor, out=lamv[:, 2, :], in0=lamv[:, 2, :], in1=lamv[:, 3, :], op=ALU.mult), reads=[blamv], writes=[blamv])
            kb.op("dve", V(nc.vector.tensor_reduce, out=lamt[:, 0:1], in_=lamv[:, 0, :], axis=AX.X, op=ALU.add), reads=[blamv], writes=[blam])
            kb.op("dve", V(nc.vector.tensor_reduce, out=lamt[:, 1:2], in_=lamv[:, 2, :], axis=AX.X, op=ALU.add), reads=[blamv], writes=[blam])
            kb.op("act", V(nc.scalar.activation, out=lamt[:, 0:2], in_=lamt[:, 0:2], func=AF.Exp), reads=[blam], writes=[blam])
            kb.op("dve", V(nc.vector.tensor_tensor, out=lamt[:, 2:3], in0=lamt[:, 1:2], in1=lamt[:, 0:1], op=ALU.subtract), reads=[blam], writes=[blam])
            kb.op("dve", V(nc.vector.tensor_scalar, out=lamt[:, 4 + l:5 + l], in0=lamt[:, 2:3], scalar1=-lam0, scalar2=None, op0=ALU.add),
                  reads=[blam], writes=[blam])
            ld("sp", bgsub, gsub[:], PB(I["subln_g"][l]), writes=[bgsub])
            kb.op("dve", V(nc.vector.tensor_scalar, out=gsub[:], in0=gsub[:], scalar1=1.0 - lam0, scalar2=None, op0=ALU.mult), reads=[bgsub], writes=[bgsub])
            ld("sp", bcw, cw[:], I["conv_w"][l].rearrange("j (ch p) -> p j ch", p=128), writes=[bcw], slow=True)
            ld("sp", bcb, cbv[:], I["conv_b"][l].rearrange("(ch p) -> p ch", p=128), writes=[bcb], slow=True)

            with ExitStack() as front:
                uT = tb(front, "uT", [128, 4, 2048], BF16); buT = [Buf("u%d" % t) for t in range(16)]
                qT = tb(front, "qT", [128, 4, 2048], BF16); bq = [Buf("q%d" % t) for t in range(16)]
                boN = [Buf("oN%d" % t) for t in range(17)]
                zs = tb(front, "zs", [128, 2048]); bzs = Buf("zs")
                zsh = tb(front, "zsh", [128, 384]); bzsh = Buf("zsh")

                with ExitStack() as es:
                    wA = tb(es, "wA", [128, 8, 2048], BF16); bwA = Buf("wA")
                    wh = tb(es, "wh", [128, 8, 384], BF16); bwh = Buf("wh")
                    kTs = tb(es, "kTs", [128, 4, 256], BF16); bkTs = Buf("kTs")
                    vst = tb(es, "vst", [128, 512], BF16); bvst = Buf("vst")
                    utl = tb(es, "utl", [128, 4, 16, 15]); butl = Buf("utl")
                    for q4 in range(4):
                        wmat(wA[:, :, q4 * 512:(q4 + 1) * 512], "w_in", l, bwA, q4 * 512, (q4 + 1) * 512)
                    wmat(wh[:], "w_in_h", l, bwh)
                    gm = gvec[:, l, 0, :]
                    for t in range(16):
                        norm_tile(t, gm)
                        hsl = lambda kc: hTt[:, hs["i"], kc, :]
                        for ch in range(12):
                            ps, bp = nps()
                            for kc in range(8):
                                mm(ps[:, 0:128], wA[:, kc, ch * 128:(ch + 1) * 128], hsl(kc), kc == 0, kc == 7, [bwA, bhts[hs["i"]]], [bp])
                            if ch < 4:
                                evac(uT[:, ch, t * 128:(t + 1) * 128], ps[:, 0:128], [bp], [buT[t]], eng="act")
                                evac(utl[:, ch, t, :], ps[:, 113:128], [bp], [butl], eng="act")
                            elif ch < 8:
                                evac(qT[:, ch - 4, t * 128:(t + 1) * 128], ps[:, 0:128], [bp], [bq[t]])
                            else:
                                evac(kTs[:, ch - 8, (t % 2) * 128:(t % 2 + 1) * 128], ps[:, 0:128], [bp], [bkTs])
                        if t % 2 == 1:
                            t0 = (t // 2) * 256
                            for hp in range(2):
                                ld("sp", bkTs, sendK[hp].rearrange("(h p) n -> p h n", p=128)[:, :, t0:t0 + 256], kTs[:, 2 * hp:2 * hp + 2, :],
                                   reads=[bkTs], writes=[bsK[hp]])
                        st, bs = nstg()
                        for kv in range(2):
                            ps, bp = nps()
                            for kc in range(8):
                                mm(ps[:], hsl(kc), wA[:, kc, 1024 + kv * 512:1536 + kv * 512], kc == 0, kc == 7, [bwA, bhts[hs["i"]]], [bp])
                            evac(st[:, kv * 512:(kv + 1) * 512], ps[:], [bp], [bs], eng="dve")
                            if kv == 1:
                                evac(vst[:], ps[:], [bp], [bvst], eng="dve")
                        ld("sp", bs, O["k_p"][l, t * 128:(t + 1) * 128, :], st[:, 0:512], reads=[bs], writes=[bout])
                        ld("sp", bs, O["v_p"][l, t * 128:(t + 1) * 128, :], st[:, 512:1024], reads=[bs], writes=[bout])
                        ld("sp", bvst, sendV[t // 8][(t % 8) * 128:(t % 8 + 1) * 128, :], vst[:], reads=[bvst], writes=[bsV[t // 8]])
                    norm_tile(16, gm)
                    for c4 in range(4):
                        ps, bp = nps()
                        for kc in range(8):
                            mm(ps[:], hTt[:, hs["i"], kc, :], wA[:, kc, c4 * 512:(c4 + 1) * 512], kc == 0, kc == 7, [bwA, bhts[hs["i"]]], [bp])
                        evac(zs[:, c4 * 512:(c4 + 1) * 512], ps[:], [bp], [bzs])
                    ps, bp = nps()
                    for kc in range(8):
                        mm(ps[:, 0:384], hTt[:, hs["i"], kc, :], wh[:, kc, :], kc == 0, kc == 7, [bwh, bhts[hs["i"]]], [bp])
                    evac(zsh[:], ps[:, 0:384], [bp], [bzsh])
                    ld("sp", bzs, O["k_s"][l], zs[:, 1024:1536], reads=[bzs], writes=[bout])
                    ld("sp", bzs, O["v_s"][l], zs[:, 1536:2048], reads=[bzs], writes=[bout])
                    ld("sp", bzs, O["pool_s"][l][:, 14 * 512:15 * 512], zs[:, 0:512], reads=[bzs], writes=[bout])
                    ld("sp", bout, O["pool_s"][l][:, 0:14 * 512], I["spool"][l][:, 512:15 * 512], writes=[bout])
                    for ch in range(4):
                        ld("sp", butl, O["pool_p"][l][:, ch * 128:(ch + 1) * 128].rearrange("t p -> p t"), utl[:, ch, 15, :],
                           reads=[butl], writes=[bout], slow=True)
                    ld("sp", butl, sendU, utl[:].rearrange("p c t n -> p (c t n)"), reads=[butl], writes=[bsU])
                    for i2 in range(2):
                        allgather(sendK[i2], recvK[i2], bsK[i2], brK[i2], G4)
                        allgather(sendV[i2], recvV[i2], bsV[i2], brV[i2], G4)
                    allgather(sendU, recvU, bsU, brU, G4)
                    kb.barrier()
                if STOP <= 1:
                    break

                with ExitStack() as es:
                    kq = tb(es, "kq", [128, 4, 2048]); bkq = [Buf("kq%d" % i) for i in range(4)]
                    prod = tb(es, "prod", [128, 2, 2048]); bprod = [Buf("prod0"), Buf("prod1")]
                    Ssc = tb(es, "Ssc", [128, 1025, 2]); bS = Buf("S")
                    alis = tb(es, "alis", [128, 1025]); balis = Buf("alis")
                    accs = tb(es, "accs", [128, 258]); bacc = Buf("accs")
                    part = tb(es, "part", [128, 2, 128]); bpart = Buf("part")
                    ld("sp", balis, alis[:], I["alibi_s"], writes=[balis])
                    qh_b = BC(zsh[:, 0:128], 1, 16)
                    tasks = [("k", sl, qt) for sl in range(8) for qt in range(8)] + [("v", sl, qt) for sl in range(8) for qt in range(8)]
                    DPF = 3

                    def gather(ti):
                        kind, sl, qt = tasks[ti]
                        i = ti % 4
                        kb.dma("pool", bkq[i], V(nc.gpsimd.indirect_dma_start, out=kq[:, i, :], out_offset=None,
                                                 in_=I[("ck%d" if kind == "k" else "cv%d") % l],
                                                 in_offset=bass.IndirectOffsetOnAxis(ap=idxs[:, sl, qt:qt + 1], axis=0)),
                               reads=[bidx], writes=[bkq[i]])

                    for ti in range(DPF):
                        gather(ti)
                    for ti in range(64):
                        kind, sl, qt = tasks[ti]
                        i = ti % 4
                        j2 = ti % 2
                        if ti + DPF < len(tasks):
                            gather(ti + DPF)
                        kb.op("pool", V(nc.gpsimd.tensor_tensor, out=prod[:, j2, :].rearrange("p (n d) -> p n d", d=128),
                                        in0=kq[:, i, :].rearrange("p (n d) -> p n d", d=128), in1=qh_b, op=ALU.mult),
                              reads=[bkq[i], bzsh], writes=[bprod[j2]])
                        p0 = sl * 128 + qt * 16
                        kb.op("dve", V(nc.vector.tensor_reduce, out=Ssc[:, p0:p0 + 16, :],
                                       in_=prod[:, j2, :].rearrange("p (n c d) -> p n c d", c=2, d=64), axis=AX.X, op=ALU.add),
                              reads=[bprod[j2]], writes=[bS])
                    kb.op("pool", V(nc.gpsimd.tensor_tensor, out=prod[:, 0, 0:128], in0=zsh[:, 0:128], in1=zsh[:, 128:256], op=ALU.mult),
                          reads=[bzsh], writes=[bprod[0]])
                    kb.op("dve", V(nc.vector.tensor_reduce, out=Ssc[:, 1024, :], in_=prod[:, 0, 0:128].rearrange("p (c d) -> p c d", c=2),
                                   axis=AX.X, op=ALU.add), reads=[bprod[0]], writes=[bS])
                    kb.op("dve", V(nc.vector.scalar_tensor_tensor, out=Ssc[:], in0=Ssc[:], scalar=0.125, in1=BC(alis[:], 2, 2),
                                   op0=ALU.mult, op1=ALU.add), reads=[bS, balis], writes=[bS])
                    kb.op("act", V(nc.scalar.activation, out=Ssc[:], in_=Ssc[:], func=AF.Exp), reads=[bS], writes=[bS])
                    kb.op("dve", V(nc.vector.tensor_scalar, out=Ssc[:, 1024, :], in0=Ssc[:, 1024, :], scalar1=isnew[:, 0:1], scalar2=None,
                                   op0=ALU.mult), reads=[bS, bisn], writes=[bS])
                    for c in range(2):
                        kb.op("dve", V(nc.vector.tensor_reduce, out=accs[:, c * 129 + 128:c * 129 + 129], in_=Ssc[:, :, c], axis=AX.X, op=ALU.add),
                              reads=[bS], writes=[bacc])
                        kb.op("dve", V(nc.vector.tensor_scalar, out=accs[:, c * 129:c * 129 + 128], in0=zsh[:, 256:384],
                                       scalar1=Ssc[:, 1024, c:c + 1], scalar2=None, op0=ALU.mult), reads=[bS, bzsh], writes=[bacc])
                    for ti in range(64, 128):
                        kind, sl, qt = tasks[ti]
                        i = ti % 4
                        if ti + DPF < len(tasks):
                            gather(ti + DPF)
                        p0 = sl * 128 + qt * 16
                        for c in range(2):
                            kb.op("pool", V(nc.gpsimd.tensor_tensor, out=prod[:, c, :].rearrange("p (n d) -> p n d", d=128),
                                            in0=kq[:, i, :].rearrange("p (n d) -> p n d", d=128),
                                            in1=BC(Ssc[:, p0:p0 + 16, c], 2, 128), op=ALU.mult), reads=[bkq[i], bS], writes=[bprod[c]])
                            kb.op("dve", V(nc.vector.tensor_reduce, out=part[:, c, :], in_=prod[:, c, :].rearrange("p (n d) -> p d n", d=128),
                                           axis=AX.X, op=ALU.add), reads=[bprod[c]], writes=[bpart])
                            kb.op("dve", V(nc.vector.tensor_tensor, out=accs[:, c * 129:c * 129 + 128], in0=accs[:, c * 129:c * 129 + 128],
                                           in1=part[:, c, :], op=ALU.add), reads=[bpart, bacc], writes=[bacc])
                    ld("sp", bacc, sendS, accs[:], reads=[bacc], writes=[bsS])
                    allgather(sendS, recvS, bsS, brS, G4)
                    kb.barrier()
                if STOP <= 2:
                    break

                front2 = front.enter_context(ExitStack())
                oN = tb(front2, "oN", [128, 17, 512], BF16)
                with ExitStack() as es:
                    rr["n"] = 3
                    rr["ps"] = 0
                    kTh_ = tb(es, "kTh", [128, 64, 128], BF16); bkv = Buf("kvb")
                    vh_ = tb(es, "vh", [128, 64, 130], BF16)
                    kTh = kTh_[:]
                    vh = vh_[:]
                    NSL = 6
                    LOOK = 3
                    pT_ = tb(es, "pT", [128, NSL, 4, 128], BF16)
                    bpT = [[Buf("pT%d_%d" % (a, b_)) for b_ in range(4)] for a in range(NSL)]
                    kb.op("pool", V(nc.gpsimd.memset, vh_[:, :, 128:130], 1.0), writes=[bkv])
                    for h in range(4):
                        for r in range(4):
                            ld("sp", bkv, kTh.rearrange("p (m r) n -> p r m n", r=4)[:, r],
                               recvK[h // 2][r * 256 + (h % 2) * 128:r * 256 + (h % 2 + 1) * 128, :].rearrange("p (m n) -> p m n", n=128),
                               reads=[brK[h // 2]], writes=[bkv])
                            for th in range(2):
                                ld("sp", bkv, vh.rearrange("p (m r) n -> p r m n", r=4)[:, r, 8 * th:8 * th + 8, 0:128],
                                   recvV[th][r * 1024:(r + 1) * 1024, h * 128:(h + 1) * 128].rearrange("(m p) e -> p m e", p=128),
                                   reads=[brV[th]], writes=[bkv])
                        kb.op("pool", V(nc.gpsimd.memset, vh_[:, :, 128:130], 1.0), writes=[bkv])
                        if h >= 1:
                            wb = BC(BC(wtab[:, h * 4:(h + 1) * 4], 1, 16), 3, 130)
                            vh4 = vh_[:].rearrange("p (g i) n -> p g i n", i=4)
                            kb.op("pool", V(nc.gpsimd.tensor_tensor, out=vh4, in0=vh4, in1=wb, op=ALU.mult), reads=[bkv, bwt], writes=[bkv])
                        glist = [(m, g0, c) for m in range(16) for g0 in range(0, 4 * m + 4, 4) for c in range(2)]
                        slope_h = 2.0 ** (-2.0 * (h + 1))

                        def emit_scores(gi):
                            m, g0, c = glist[gi]
                            sl = gi % NSL
                            ps, bp = nps()
                            for i4 in range(4):
                                kt = g0 + i4
                                mm(ps[:, i4 * 128:(i4 + 1) * 128], kTh[c * 64:(c + 1) * 64, kt, :],
                                   qT[c * 64:(c + 1) * 64, h, m * 128:(m + 1) * 128], True, True, [bkv, bq[m]], [bp])
                            if h >= 1:
                                kb.op("act", V(nc.scalar.activation, out=pT_[:, sl].rearrange("p i n -> p (i n)"), in_=ps[:, 0:512], func=AF.Exp,
                                               bias=float(slope_h * 128.0 * (g0 - 4 * m)), scale=0.125), reads=[bp], writes=bpT[sl])
                            for i4 in range(4):
                                kt = g0 + i4
                                e = 4 * m + 3 - kt
                                if h == 0:
                                    kb.op("act", V(nc.scalar.activation, out=pT_[:, sl, i4, :], in_=ps[:, i4 * 128:(i4 + 1) * 128], func=AF.Exp,
                                                   bias=alip[:, h * 64 + e:h * 64 + e + 1], scale=0.125), reads=[bp, balip], writes=[bpT[sl][i4]])
                                if e < 4:
                                    kb.op("dve", V(nc.vector.tensor_tensor, out=pT_[:, sl, i4, :], in0=pT_[:, sl, i4, :], in1=maskt[:, e, :],
                                                   op=ALU.mult), reads=[bpT[sl][i4], bmask], writes=[bpT[sl][i4]])

                        def emit_pv(gi):
                            m, g0, c = glist[gi]
                            sl = gi % NSL
                            nk = 4 * m + 4
                            pi = 3 + 2 * (m % 2)
                            pso, bpo = psl[pi + c], bps[pi + c]
                            for i4 in range(4):
                                kt = g0 + i4
                                mm(pso[:, 0:129], pT_[:, sl, i4, :], vh[:, kt, 0:129], kt == 0, kt == nk - 1, [bpT[sl][i4], bkv], [bpo])
                            if g0 + 4 == nk and c == 1:
                                p1, b1, p2, b2 = psl[pi], bps[pi], psl[pi + 1], bps[pi + 1]
                                diff_combine(p1[:, 0:128], p2[:, 0:128], p1[:, 128:129], p2[:, 128:129],
                                             oN[:, m, h * 128:(h + 1) * 128], [b1, b2], [boN[m]], l)

                        for gi in range(len(glist) + LOOK):
                            if gi < len(glist):
                                emit_scores(gi)
                            if gi >= LOOK:
                                emit_pv(gi - LOOK)
                    rr["n"] = 7
                    kb.barrier()
                if STOP <= 3:
                    break

                with ExitStack() as es:
                    Rm = tb(es, "Rm", [128, 2, 4, 258]); bRm = Buf("Rm")
                    tot = tb(es, "tot", [128, 4, 258]); btot = Buf("tot")
                    rS = recvS.rearrange("(r p) c -> p r c", p=128)
                    ld("sp", bRm, Rm[:, 0], rS, reads=[brS], writes=[bRm])
                    ld("sp", bRm, Rm[0:64, 1], rS[64:128], reads=[brS], writes=[bRm])
                    ld("sp", bRm, Rm[64:128, 1], rS[0:64], reads=[brS], writes=[bRm])
                    kb.op("dve", V(nc.vector.tensor_tensor, out=tot[:], in0=Rm[:, 0], in1=Rm[:, 1], op=ALU.add), reads=[bRm], writes=[btot])
                    for hh in range(4):
                        diff_combine(tot[:, hh, 0:128], tot[:, hh, 129:257], tot[:, hh, 128:129], tot[:, hh, 257:258],
                                     oN[:, 16, hh * 128:(hh + 1) * 128], [btot], [boN[16]], l)
                    kb.barrier()

                with ExitStack() as es:
                    wB = tb(es, "wB", [128, 8, 1024], BF16); bwB = Buf("wB")
                    pw = tb(es, "pw", [128, 4, 128], BF16); bpw = Buf("pw")
                    wmat(wB[:], "w_out", l, bwB)
                    wload(pw[:], I["pool_w"][l].rearrange("g c d -> c g d"), bpw)
                    hsel_, bhu = halo_select(es, recvU, brU, 960, 15, "hu")
                    hsel = hsel_[:].rearrange("p (c t n) -> p c t n", t=16, n=15)
                    ext = tb(es, "ext", [128, 4, 4, 144]); bext = Buf("ext")
                    ptm = tb(es, "ptm", [128, 4, 128], BF16); bptm = Buf("ptm")
                    mixT = tb(es, "mixT", [128, 8, 128], BF16); bmix = Buf("mixT")
                    splg = tb(es, "splg", [128, 15, 128]); bspl = Buf("splg")
                    ptk = tb(es, "ptk", [128, 512]); bptk = Buf("ptk")
                    for t in range(17):
                        if t < 16:
                            kb.op("pool", V(nc.gpsimd.tensor_copy, out=ext[:, 0, :, 0:15], in_=hsel[:, :, t, :]), reads=[bhu], writes=[bext])
                            kb.op("pool", V(nc.gpsimd.tensor_copy, out=ext[:, 0, :, 15:143], in_=uT[:, :, t * 128:(t + 1) * 128]),
                                  reads=[buT[t]], writes=[bext])
                            kb.op("dve", V(nc.vector.tensor_tensor, out=ext[:, 1, :, 1:143], in0=ext[:, 0, :, 1:143], in1=ext[:, 0, :, 0:142], op=ALU.add),
                                  reads=[bext], writes=[bext])
                            kb.op("dve", V(nc.vector.tensor_tensor, out=ext[:, 2, :, 3:143], in0=ext[:, 1, :, 3:143], in1=ext[:, 1, :, 1:141], op=ALU.add),
                                  reads=[bext], writes=[bext])
                            kb.op("dve", V(nc.vector.tensor_tensor, out=ext[:, 3, :, 7:143], in0=ext[:, 2, :, 7:143], in1=ext[:, 2, :, 3:139], op=ALU.add),
                                  reads=[bext], writes=[bext])
                            kb.op("dve", V(nc.vector.tensor_tensor, out=ext[:, 1, 3, 15:143], in0=ext[:, 3, 3, 15:143], in1=ext[:, 3, 3, 7:135], op=ALU.add),
                                  reads=[bext], writes=[bext])
                            wsrc = [ext[:, 1, 0, 15:143], ext[:, 2, 1, 15:143], ext[:, 3, 2, 15:143], ext[:, 1, 3, 15:143]]
                            for g in range(4):
                                if t == 0:
                                    kb.op("dve", V(nc.vector.tensor_tensor, out=wsrc[g], in0=wsrc[g], in1=invc0[:, g, :], op=ALU.mult),
                                          reads=[bext, binv], writes=[bext])
                                    kb.op("dve", V(nc.vector.tensor_tensor, out=ptm[:, g, :], in0=wsrc[g], in1=ext[:, 0, g, 15:143], op=ALU.subtract),
                                          reads=[bext], writes=[bptm])
                                else:
                                    kb.op("dve", V(nc.vector.scalar_tensor_tensor, out=ptm[:, g, :], in0=wsrc[g], scalar=1.0 / (2 << g),
                                                   in1=ext[:, 0, g, 15:143], op0=ALU.mult, op1=ALU.subtract), reads=[bext], writes=[bptm])
                        else:
                            for g in range(4):
                                w = 2 << g
                                ld("sp", bspl, splg[:], I["spool"][l].rearrange("p (r c) -> p r c", c=512)[:, :, g * 128:(g + 1) * 128], writes=[bspl])
                                kb.op("dve", V(nc.vector.tensor_reduce, out=ptk[:, g * 128:(g + 1) * 128],
                                               in_=splg[:].rearrange("p r c -> p c r")[:, :, 15 - (w - 1):15], axis=AX.X, op=ALU.add),
                                      reads=[bspl], writes=[bptk])
                                kb.op("dve", V(nc.vector.tensor_tensor, out=ptk[:, g * 128:(g + 1) * 128], in0=ptk[:, g * 128:(g + 1) * 128],
                                               in1=zs[:, g * 128:(g + 1) * 128], op=ALU.add), reads=[bptk, bzs], writes=[bptk])
                                kb.op("dve", V(nc.vector.scalar_tensor_tensor, out=ptk[:, g * 128:(g + 1) * 128], in0=ptk[:, g * 128:(g + 1) * 128],
                                               scalar=1.0 / w, in1=zs[:, g * 128:(g + 1) * 128], op0=ALU.mult, op1=ALU.subtract),
                                      reads=[bptk, bzs], writes=[bptk])
                            ps, bp = nps()
                            for g in range(4):
                                tr(ps[:, g * 128:(g + 1) * 128], ptk[:, g * 128:(g + 1) * 128], ident_f[:], [bptk, bidf], [bp])
                            evac(ptm[:], ps[:].rearrange("p (g n) -> p g n", g=4), [bp], [bptm])
                        for g in range(4):
                            ps, bp = nps()
                            mm(ps[:, 0:128], pw[:, g, :], ptm[:, g, :], True, True, [bpw, bptm], [bp])
                            kb.op("dve", V(nc.vector.tensor_scalar, out=mixT[:, g, :], in0=ps[:, 0:128], scalar1=psc[:, l, g:g + 1], scalar2=None,
                                           op0=ALU.mult), reads=[bp, bpsc], writes=[bmix])
                        for hh in range(4):
                            tr(psb[:, hh * 128:(hh + 1) * 128], oN[:, t, hh * 128:(hh + 1) * 128], ident_b[:], [boN[t], bidb], [bpsb])
                        evac(mixT[:, 4:8, :], psb[:, 0:512].rearrange("p (g n) -> p g n", g=4), [bpsb], [bmix])
                        for dc in range(8):
                            ps, bp = nps()
                            for kc in range(8):
                                mm(ps[:, 0:128], wB[:, kc, dc * 128:(dc + 1) * 128], mixT[:, kc, :], kc == 0, kc == 7, [bwB, bmix], [bp])
                            kb.op("dve", V(nc.vector.tensor_tensor, out=xT[:, dc, t * 128:(t + 1) * 128], in0=xT[:, dc, t * 128:(t + 1) * 128],
                                           in1=ps[:, 0:128], op=ALU.add), reads=[bp, bx[t]], writes=[bx[t]])
                    kb.barrier()
            if STOP <= 4:
                break

            with ExitStack() as es:
                wQ = tb(es, "wQ", [128, 8, 1024], BF16); bwQ = Buf("wQ")
                wO = tb(es, "wO", [128, 8, 1024], BF16); bwO = Buf("wO")
                memKT = tb(es, "memKT", [128, 8, 256], BF16); bmk = Buf("memKT")
                memV1 = tb(es, "memV1", [128, 2, 4, 272], BF16); bmv = Buf("memV1")
                qcT = tb(es, "qcT", [128, 8, 128], BF16); bqc = Buf("qcT")
                PcT = tb(es, "PcT", [128, 2, 128], BF16); bPc = Buf("PcT")
                ocn = tb(es, "ocn", [128, 1024], BF16); bocn = Buf("ocn")
                wqh = tb(es, "wqh", [128, 8, 256], BF16); bwqh = Buf("wqh")
                qch = tb(es, "qch", [128, 256]); bqch = Buf("qch")
                ck(4.05)
                wmat(wQ[:], "wq_c", l, bwQ)
                wmat(wqh[:], "wq_ch", l, bwqh)
                ck(4.1)
                with ExitStack() as es2:
                    wKV = tb(es2, "wKV", [128, 8, 1024], BF16); bwKV = Buf("wKV")
                    for kv in range(2):
                        wmat(wKV[:], "wk_c" if kv == 0 else "wv_c", l, bwKV)
                        for mh in range(2):
                            st, bs = nstg()
                            for hf in range(2):
                                ps, bp = nps()
                                for kc in range(8):
                                    mm(ps[:], mpT[:, kc, mh * 128:(mh + 1) * 128], wKV[:, kc, hf * 512:(hf + 1) * 512], kc == 0, kc == 7, [bwKV, bmpT], [bp])
                                evac(st[:, hf * 512:(hf + 1) * 512], ps[:], [bp], [bs], eng="dve")
                                if kv == 1 and not os.environ.get("KNOMV"):
                                    for h2 in range(2):
                                        evac(memV1[:, mh, hf * 2 + h2, 0:256], ps[:, h2 * 256:(h2 + 1) * 256], [bp], [bmv], eng="dve")
                            ld("sp", bs, O["memk" if kv == 0 else "memv"][l, mh * 128:(mh + 1) * 128, :], st, reads=[bs], writes=[bout])
                        ck(4.15 + 0.1 * kv)
                        if kv == 0:
                            for dc in range(8):
                                ps, bp = nps()
                                for kc in range(8):
                                    mm(ps[:, 0:256], wKV[:, kc, dc * 128:(dc + 1) * 128], mpT[:, kc, :], kc == 0, kc == 7, [bwKV, bmpT], [bp])
                                evac(memKT[:, dc, :], ps[:, 0:256], [bp], [bmk])
                            ck(4.2)
                    for mh_ in range(2):
                        for hc_ in range(4):
                            kb.op("pool", V(nc.gpsimd.memset, memV1[:, mh_, hc_, 256:257], 1.0), writes=[bmv])
                    kb.barrier()
                ck(4.3)
                wmat(wO[:], "wo_c", l, bwO)
                for t in range(16):
                    norm_tile(t, gvec[:, l, 1, :])
                    for dc in range(8):
                        ps, bp = nps()
                        for kc in range(8):
                            mm(ps[:, 0:128], wQ[:, kc, dc * 128:(dc + 1) * 128], hTt[:, hs["i"], kc, :], kc == 0, kc == 7, [bwQ, bhts[hs["i"]]], [bp])
                        evac(qcT[:, dc, :], ps[:, 0:128], [bp], [bqc])
                    for hc in range(4):
                        ps, bp = nps()
                        for mc in range(2):
                            for dch in range(2):
                                mm(ps[:, mc * 128:(mc + 1) * 128], memKT[:, hc * 2 + dch, mc * 128:(mc + 1) * 128], qcT[:, hc * 2 + dch, :],
                                   dch == 0, dch == 1, [bmk, bqc], [bp])
                        kb.op("act", V(nc.scalar.activation, out=PcT[:], in_=ps[:, 0:256].rearrange("p (m n) -> p m n", m=2), func=AF.Exp,
                                       scale=1.0 / 16.0), reads=[bp], writes=[bPc])
                        ps2, bp2 = nps()
                        for mc in range(2):
                            mm(ps2[:, 0:257], PcT[:, mc, :], memV1[:, mc, hc, 0:257], mc == 0, mc == 1, [bPc, bmv], [bp2])
                        kb.op("dve", V(nc.vector.reciprocal, out=small[:, 8:9], in_=ps2[:, 256:257]), reads=[bp2], writes=[bsm])
                        kb.op("dve", V(nc.vector.tensor_scalar, out=ocn[:, hc * 256:(hc + 1) * 256], in0=ps2[:, 0:256], scalar1=small[:, 8:9],
                                       scalar2=None, op0=ALU.mult), reads=[bp2, bsm], writes=[bocn])
                    for dc in range(8):
                        tr(psb[:, dc * 128:(dc + 1) * 128], ocn[:, dc * 128:(dc + 1) * 128], ident_b[:], [bocn, bidb], [bpsb])
                    evac(qcT[:], psb[:].rearrange("p (g n) -> p g n", g=8), [bpsb], [bqc])
                    for dc in range(8):
                        ps, bp = nps()
                        for kc in range(8):
                            mm(ps[:, 0:128], wO[:, kc, dc * 128:(dc + 1) * 128], qcT[:, kc, :], kc == 0, kc == 7, [bwO, bqc], [bp])
                        kb.op("dve", V(nc.vector.tensor_tensor, out=xT[:, dc, t * 128:(t + 1) * 128], in0=xT[:, dc, t * 128:(t + 1) * 128],
                                       in1=ps[:, 0:128], op=ALU.add), reads=[bp, bx[t]], writes=[bx[t]])
                ck(4.5)
                xtl = tb(es, "xtl", [128, 8, 16, 2]); bxtl = Buf("xtl")
                kb.op("dve", V(nc.vector.tensor_copy, out=xtl[:], in_=xT[:, :, 0:2048].rearrange("p c (t n) -> p c t n", n=128)[:, :, :, 126:128]),
                      reads=bx[0:16], writes=[bxtl])
                ld("sp", bxtl, sendX, xtl[:].rearrange("p c t n -> p (c t n)"), reads=[bxtl], writes=[bsX])
                allgather(sendX, recvX, bsX, brX, G4)
                ck(4.6)
                with ExitStack() as es2:
                    kq = tb(es2, "kq", [128, 2, 2048]); bkq = [Buf("kq0"), Buf("kq1")]
                    prod = tb(es2, "prod", [128, 2, 2048]); bprod = [Buf("prod0"), Buf("prod1")]
                    Sc = tb(es2, "Sc", [128, 128]); bS = Buf("Sc")
                    accs = tb(es2, "accs", [128, 257]); bacc = Buf("accs")
                    part = tb(es2, "part", [128, 256]); bpart = Buf("part")
                    Rm = tb(es2, "Rm", [128, 2, 4, 257]); bRm = Buf("Rm")
                    tot = tb(es2, "tot", [128, 4, 257]); btot = Buf("tot")
                    norm_tile(16, gvec[:, l, 1, :])
                    ps, bp = nps()
                    for kc in range(8):
                        mm(ps[:, 0:256], hTt[:, hs["i"], kc, :], wqh[:, kc, :], kc == 0, kc == 7, [bwqh, bhts[hs["i"]]], [bp])
                    evac(qch[:], ps[:, 0:256], [bp], [bqch])
                    qc_b = BC(qch[:], 1, 8)
                    for pc in range(16):
                        i = pc % 2
                        ld("sp", bkq[i], kq[:, i, :], I["cmk"][l][:, pc * 2048:(pc + 1) * 2048], writes=[bkq[i]])
                        kb.op("pool", V(nc.gpsimd.tensor_tensor, out=prod[:, i, :].rearrange("p (n d) -> p n d", d=256),
                                        in0=kq[:, i, :].rearrange("p (n d) -> p n d", d=256), in1=qc_b, op=ALU.mult),
                              reads=[bkq[i], bqch], writes=[bprod[i]])
                        kb.op("dve", V(nc.vector.tensor_reduce, out=Sc[:, pc * 8:pc * 8 + 8], in_=prod[:, i, :].rearrange("p (n d) -> p n d", d=256),
                                       axis=AX.X, op=ALU.add), reads=[bprod[i]], writes=[bS])
                    kb.op("dve", V(nc.vector.memset, accs[:], 0.0), writes=[bacc])
                    kb.op("act", V(nc.scalar.activation, out=Sc[:], in_=Sc[:], func=AF.Exp, scale=1.0 / 16.0, accum_out=accs[:, 256:257]),
                          reads=[bS, bacc], writes=[bS, bacc])
                    for pc in range(16):
                        i = pc % 2
                        ld("sp", bkq[i], kq[:, i, :], I["cmv"][l][:, pc * 2048:(pc + 1) * 2048], writes=[bkq[i]])
                        kb.op("pool", V(nc.gpsimd.tensor_tensor, out=prod[:, i, :].rearrange("p (n d) -> p n d", d=256),
                                        in0=kq[:, i, :].rearrange("p (n d) -> p n d", d=256), in1=BC(Sc[:, pc * 8:pc * 8 + 8], 2, 256), op=ALU.mult),
                              reads=[bkq[i], bS], writes=[bprod[i]])
                        kb.op("dve", V(nc.vector.tensor_reduce, out=part[:], in_=prod[:, i, :].rearrange("p (n d) -> p d n", d=256),
                                       axis=AX.X, op=ALU.add), reads=[bprod[i]], writes=[bpart])
                        kb.op("dve", V(nc.vector.tensor_tensor, out=accs[:, 0:256], in0=accs[:, 0:256], in1=part[:], op=ALU.add),
                              reads=[bpart, bacc], writes=[bacc])
                    ld("sp", bacc, sendC, accs[:], reads=[bacc], writes=[bsC])
                    allgather(sendC, recvC, bsC, brC, G4)
                    rC = recvC.rearrange("(r p) c -> p r c", p=128)
                    ld("sp", bRm, Rm[:, 0], rC, reads=[brC], writes=[bRm])
                    ld("sp", bRm, Rm[0:64, 1], rC[64:128], reads=[brC], writes=[bRm])
                    ld("sp", bRm, Rm[64:128, 1], rC[0:64], reads=[brC], writes=[bRm])
                    kb.op("dve", V(nc.vector.tensor_tensor, out=tot[:], in0=Rm[:, 0], in1=Rm[:, 1], op=ALU.add), reads=[bRm], writes=[btot])
                    for hc in range(4):
                        kb.op("dve", V(nc.vector.reciprocal, out=small[:, 8:9], in_=tot[:, hc, 256:257]), reads=[btot], writes=[bsm])
                        kb.op("dve", V(nc.vector.tensor_scalar, out=ocn[:, hc * 256:(hc + 1) * 256], in0=tot[:, hc, 0:256], scalar1=small[:, 8:9],
                                       scalar2=None, op0=ALU.mult), reads=[btot, bsm], writes=[bocn])
                    for dc in range(8):
                        tr(psb[:, dc * 128:(dc + 1) * 128], ocn[:, dc * 128:(dc + 1) * 128], ident_b[:], [bocn, bidb], [bpsb])
                    evac(qcT[:], psb[:].rearrange("p (g n) -> p g n", g=8), [bpsb], [bqc])
                    for dc in range(8):
                        ps, bp = nps()
                        for kc in range(8):
                            mm(ps[:, 0:128], wO[:, kc, dc * 128:(dc + 1) * 128], qcT[:, kc, :], kc == 0, kc == 7, [bwO, bqc], [bp])
                        kb.op("dve", V(nc.vector.tensor_tensor, out=xT[:, dc, 2048:2176], in0=xT[:, dc, 2048:2176], in1=ps[:, 0:128], op=ALU.add),
                              reads=[bp, bx[16]], writes=[bx[16]])
                    kb.barrier()
                ck(4.8)
                hx_, bhx = halo_select(es, recvX, brX, 256, 2, "hx")
                kb.op("dve", V(nc.vector.tensor_copy, out=xT[:, :, HB:HB + 32], in_=hx_[:].rearrange("p (c n) -> p c n", c=8)), reads=[bhx], writes=[bxh])
                kb.barrier()
            if STOP <= 5:
                break

            with ExitStack() as es:
                hT = tb(es, "hT", [128, 8, 17 * 130], BF16); bh = [Buf("h%d" % t) for t in range(17)]
                hh = tb(es, "hh", [128, 8, 32], BF16); bhh = Buf("hh")
                wup = tb(es, "wup", [128, 4, 8, 256], BF16); bwup = [Buf("wup%d" % i) for i in range(4)]
                wdn = tb(es, "wdn", [128, 4, 1024], BF16); bwdn = [Buf("wdn%d" % i) for i in range(4)]
                scs = tb(es, "scs", [128, 4, 2, 2, 128]); bscs = [Buf("scs%d" % i) for i in range(4)]
                cva = tb(es, "cva", [128, 2, 3, 128]); bcva = Buf("cva")
                cvg = tb(es, "cvg", [128, 2, 3, 128]); bcvg = Buf("cvg")
                actT = tb(es, "actT", [128, 2, 384], BF16); bact = [Buf("act0"), Buf("act1")]
                ups = tb(es, "ups", [128, 4, 256]); bups = [Buf("ups%d" % i) for i in range(4)]
                cvst = tb(es, "cvst", [128, 44, 2]); bcvst = Buf("cvst")
                gf = gvec[:, l, 2, :]
                for t in range(17):
                    norm(t * 128, 128, gf, lambda kc, t=t: hT[:, kc, t * 130 + 2:t * 130 + 130], [bx[t]], [bh[t]])
                norm(HB, 32, gf, lambda kc: hh[:, kc, :], [bxh], [bhh])
                kb.op("dve", V(nc.vector.tensor_copy, out=hT[:, :, 0:2080].rearrange("p c (t n) -> p c t n", n=130)[:, :, :, 0:2],
                               in_=hh[:].rearrange("p c (t n) -> p c t n", n=2)), reads=[bhh], writes=bh[0:16])
                kb.op("pool", V(nc.gpsimd.memset, hT[:, :, 2080:2082], 0.0), writes=[bh[16]])
                groups = [(0, 3), (3, 3), (6, 3), (9, 3), (12, 3), (15, 1), (16, 1)]
                ai = 0
                for p2 in range(0, NPAIR, 2):
                    for q_ in range(2):
                        p = p2 + q_
                        wi = p % 4
                        wmat(wup[:, wi, :, 0:128], "w_up", l, bwup[wi], p * 128, (p + 1) * 128)
                        wmat(wup[:, wi, :, 128:256], "w_up", l, bwup[wi], DFF + p * 128, DFF + (p + 1) * 128)
                        wload(wdn[:, wi, :], I["w_down"][l][p * 128:(p + 1) * 128, :], bwdn[wi])
                        for ag in range(2):
                            ld("sp", bscs[wi], scs[:, wi, ag],
                               I["sconv"][l].rearrange("p (r f) -> p r f", r=2)[:, :, ag * DFF + p * 128:ag * DFF + (p + 1) * 128], writes=[bscs[wi]])
                    for (t0, nt) in groups:
                        for q_ in range(2):
                            p = p2 + q_
                            wi = p % 4
                            ncol = nt * 130
                            c0 = t0 * 130
                            pa, bpa = nps()
                            pg, bpg = nps()
                            for ag, (pp, bpp) in enumerate(((pa, bpa), (pg, bpg))):
                                for kc in range(8):
                                    mm(pp[:, 0:ncol], wup[:, wi, kc, ag * 128:(ag + 1) * 128], hT[:, kc, c0:c0 + ncol], kc == 0, kc == 7,
                                       [bwup[wi]] + bh[t0:t0 + nt], [bpp])
                            pst = None
                            if t0 == 16:
                                pst, bpst = nps()
                                for ag in range(2):
                                    for r in range(2):
                                        tr(pst[:, (ag * 2 + r) * 128:(ag * 2 + r + 1) * 128], scs[:, wi, ag, r, :], ident_f[:], [bscs[wi], bidf], [bpst])
                            for ag, (pp, bpp, cv, bcv) in enumerate(((pa, bpa, cva, bcva), (pg, bpg, cvg, bcvg))):
                                ch = ag * NPAIR + p
                                v3 = pp[:, 0:ncol].rearrange("p (t n) -> p t n", n=130)
                                if t0 == 16:
                                    in1 = pst[:, (ag * 2 + 1) * 128:(ag * 2 + 2) * 128].rearrange("p (t n) -> p t n", t=1)
                                    in0 = pst[:, (ag * 2) * 128:(ag * 2 + 1) * 128].rearrange("p (t n) -> p t n", t=1)
                                    rdx = [bpp, bpst]
                                else:
                                    in1 = v3[:, :, 1:129]
                                    in0 = v3[:, :, 0:128]
                                    rdx = [bpp]
                                kb.op("act", V(nc.scalar.activation, out=cv[:, 0, 0:nt, :], in_=v3[:, :, 2:130], func=AF.Identity,
                                               bias=cbv[:, ch:ch + 1], scale=cw[:, 2, ch:ch + 1]), reads=[bpp, bcw, bcb], writes=[bcv])
                                kb.op("dve", V(nc.vector.scalar_tensor_tensor, out=cv[:, 1, 0:nt, :], in0=in1, scalar=cw[:, 1, ch:ch + 1],
                                               in1=cv[:, 0, 0:nt, :], op0=ALU.mult, op1=ALU.add), reads=rdx + [bcv, bcw], writes=[bcv])
                                kb.op("dve", V(nc.vector.scalar_tensor_tensor, out=cv[:, 0, 0:nt, :], in0=in0, scalar=cw[:, 0, ch:ch + 1],
                                               in1=cv[:, 1, 0:nt, :], op0=ALU.mult, op1=ALU.add), reads=rdx + [bcv, bcw], writes=[bcv])
                                if t0 == 15:
                                    kb.op("dve", V(nc.vector.tensor_copy, out=cvst[:, ch, :], in_=v3[:, 0, 128:130]), reads=[bpp], writes=[bcvst])
                            kb.op("act", V(nc.scalar.activation, out=cvg[:, 1, 0:nt, :], in_=cvg[:, 0, 0:nt, :], func=AF.Silu), reads=[bcvg], writes=[bcvg])
                            a_i = q_
                            pass
                            kb.op("dve", V(nc.vector.tensor_tensor, out=actT[:, a_i, 0:nt * 128].rearrange("p (t n) -> p t n", n=128),
                                           in0=cva[:, 0, 0:nt, :], in1=cvg[:, 1, 0:nt, :], op=ALU.mult), reads=[bcva, bcvg], writes=[bact[a_i]])
                            if t0 == 16:
                                ps, bp = nps()
                                for kc in range(8):
                                    mm(ps[:, 0:256], hT[:, kc, 16 * 130 + 2:17 * 130], wup[:, wi, kc, :], kc == 0, kc == 7, [bwup[wi], bh[16]], [bp])
                                evac(ups[:, wi, :], ps[:, 0:256], [bp], [bups[wi]])
                                for ag in range(2):
                                    ld("sp", bups[wi], O["conv_s"][l][:, 5632 + ag * DFF + p * 128:5632 + ag * DFF + (p + 1) * 128],
                                       ups[:, wi, ag * 128:(ag + 1) * 128], reads=[bups[wi]], writes=[bout])
                        for dc in range(8):
                            ps, bp = nps()
                            for q_ in range(2):
                                wi = (p2 + q_) % 4
                                mm(ps[:, 0:nt * 128], wdn[:, wi, dc * 128:(dc + 1) * 128], actT[:, q_, 0:nt * 128], q_ == 0, q_ == 1, [bwdn[wi], bact[q_]], [bp])
                            kb.op("dve", V(nc.vector.tensor_tensor, out=xT[:, dc, t0 * 128:(t0 + nt) * 128], in0=xT[:, dc, t0 * 128:(t0 + nt) * 128],
                                           in1=ps[:, 0:nt * 128], op=ALU.add), reads=[bp] + bx[t0:t0 + nt], writes=bx[t0:t0 + nt])
                for t_ in range(2):
                    ld("sp", bcvst, O["conv_p"][l][t_].rearrange("(ch p) -> p ch", p=128), cvst[:, :, t_], reads=[bcvst], writes=[bout], slow=True)
                ld("sp", bout, O["conv_s"][l][:, 0:5632], I["sconv"][l][:, 5632:11264], writes=[bout])
                kb.barrier()

    except _Stop:
        pass

    xn = sb("xn", [128, 8, 128]); bxn = Buf("xn")
    for t in range(17):
        norm(t * 128, 128, gfin, lambda kc: xn[:, kc, :], [bx[t]], [bxn])
        st, bs = nstg()
        for half in range(2):
            ps, bp = nps()
            for k4 in range(4):
                kc = half * 4 + k4
                tr(ps[:, k4 * 128:(k4 + 1) * 128], xn[:, kc, :], ident_f[:], [bxn, bidf], [bp])
            evac(st[:, half * 512:(half + 1) * 512], ps[:], [bp], [bs])
        dst = O["y_p"][t * 128:(t + 1) * 128, :] if t < 16 else O["y_s"]
        ld("sp", bs, dst, st, reads=[bs], writes=[bout])
    kb.barrier()
    return nc, kb


_PROG = {}


def _slopes():
    return 2.0 ** (-8.0 * np.arange(1, 5) / 4.0)


def kernel(**inp):
    inp = {k: np.asarray(v) for k, v in inp.items()}
    if "nc" not in _PROG:
        _PROG["nc"], _PROG["kb"] = build_program()
    nc = _PROG["nc"]
    f32 = np.float32
    slopes = _slopes()
    in_maps = []
    xp = inp["x_prompt"]
    pid = np.arange(128)
    for c in range(8):
        b, j = c // 4, c % 4
        h, g = c % 4, c // 4
        sq_ = np.concatenate([np.arange(64 * g, 64 * g + 64)] * 2)
        hf_ = np.repeat(np.arange(2), 64)
        m = {}
        m["xp"] = np.ascontiguousarray(xp[b].reshape(16, 4, 128, 1024)[:, j].reshape(2048, 1024))
        m["xs"] = np.ascontiguousarray(inp["x_sample"].reshape(128, 1024)[sq_])
        m["memp"] = np.ascontiguousarray(inp["mem_prompt"][b])
        for l_ in range(2):
            m["ck%d" % l_] = np.ascontiguousarray(inp["cache_k"][l_, :, :, h, :]).reshape(20480, 2048)
            m["cv%d" % l_] = np.ascontiguousarray(inp["cache_v"][l_, :, :, h, :]).reshape(20480, 2048)
        cmk_ = inp["cache_mem_k"][:, 64 * g:64 * g + 64, :, h, :].reshape(2, 64, 2, 128 * 256)
        cmv_ = inp["cache_mem_v"][:, 64 * g:64 * g + 64, :, h, :].reshape(2, 64, 2, 128 * 256)
        m["cmk"] = np.ascontiguousarray(cmk_.transpose(0, 2, 1, 3).reshape(2, 128, 32768))
        m["cmv"] = np.ascontiguousarray(cmv_.transpose(0, 2, 1, 3).reshape(2, 128, 32768))
        m["spool"] = np.ascontiguousarray(inp["state_pool"].reshape(2, 128, 7680)[:, sq_])
        m["sconv"] = np.ascontiguousarray(inp["state_conv"].reshape(2, 128, 11264)[:, sq_])
        pt_ = inp["page_table"][64 * g:64 * g + 64].reshape(64, 2, 8)
        m["ptab"] = np.ascontiguousarray(pt_.transpose(1, 0, 2).reshape(128, 8)).astype(np.int32)
        for k in ("g_mix", "w_in", "pool_w", "pool_scale", "subln_g", "w_out", "g_cross", "wq_c", "wk_c", "wv_c", "wo_c",
                  "g_ffn", "w_up", "conv_w", "conv_b", "w_down", "g_final"):
            m[k] = inp[k]
        w_in = inp["w_in"]
        m["w_in_h"] = np.ascontiguousarray(np.concatenate(
            [w_in[:, :, 512 + h * 128:512 + (h + 1) * 128], w_in[:, :, 1024 + h * 128:1024 + (h + 1) * 128],
             w_in[:, :, 1536 + h * 128:1536 + (h + 1) * 128]], axis=2))
        m["wq_ch"] = np.ascontiguousarray(inp["wq_c"][:, :, h * 256:(h + 1) * 256])
        m["lam4"] = np.ascontiguousarray(np.stack([inp["lam_q1"], inp["lam_k1"], inp["lam_q2"], inp["lam_k2"]], axis=1))
        m["ident"] = np.eye(128, dtype=f32)
        sel = np.zeros((128, 4), f32)
        sel[:, (j - 1) if j >= 1 else 3] = 1.0
        m["sel"] = sel
        md = np.zeros((128, 4, 128), f32)
        for e in range(4):
            if e > 3 - j:
                md[:, e, :] = 1.0
            elif e == 3 - j:
                md[:, e, :] = (pid[:, None] <= pid[None, :]).astype(f32)
        m["maskd"] = md
        ap_ = np.zeros((128, 4, 64), np.float64)
        for hh in range(4):
            for e in range(64):
                dd = e + j - 3
                ap_[:, hh, e] = slopes[hh] * (pid - 128.0 * dd - 64.0) if dd >= 0 else -30000.0
        m["alibi_p"] = ap_.reshape(128, 256).astype(f32)
        kpos = 1024 * hf_[:, None] + np.arange(1024)[None, :]
        als = np.zeros((128, 1025), np.float64)
        als[:, :1024] = -slopes[h] * (2048.0 - kpos)
        m["alibi_s"] = als.astype(f32)
        m["isnew"] = hf_.astype(f32).reshape(128, 1)
        ic = np.zeros((128, 4, 128), f32)
        for g in range(4):
            w = 2 << g
            ic[:, g, :] = 1.0 / w
            if j == 0:
                ic[:, g, :] = 1.0 / np.minimum(np.arange(128) + 1, w)[None, :]
        m["invc0"] = ic
        wt = np.ones((128, 4, 4), np.float64)
        for hh in range(1, 4):
            for i4 in range(4):
                wt[:, hh, i4] = np.exp(slopes[hh] * (128.0 * i4 + pid - 64.0))
        m["wtab"] = wt.reshape(128, 16).astype(f32)
        in_maps.append({k: np.ascontiguousarray(v) for k, v in m.items()})
    res = run_bass_kernel_spmd(nc, in_maps, core_ids=list(range(8)))
    R = res.results
    y_p = np.zeros((2, 64, 128, 1024), f32)
    k_p = np.zeros((2, 2, 64, 128, 512), f32)
    v_p = np.zeros((2, 2, 64, 128, 512), f32)
    for c in range(8):
        b, j = c // 4, c % 4
        y_p[b, j::4] = R[c]["y_p"].reshape(16, 128, 1024)
        k_p[:, b, j::4] = R[c]["k_p"].reshape(2, 16, 128, 512)
        v_p[:, b, j::4] = R[c]["v_p"].reshape(2, 16, 128, 512)
    y_p = y_p.reshape(2, 8192, 1024)
    k_p = k_p.reshape(2, 2, 8192, 4, 128)
    v_p = v_p.reshape(2, 2, 8192, 4, 128)
    def samp(name):
        return np.concatenate([R[0][name][..., 0:64, :], R[4][name][..., 0:64, :]], axis=-2)
    y_s = samp("y_s").reshape(128, 1, 1024)
    memk = np.stack([R[0]["memk"], R[4]["memk"]], axis=1).reshape(2, 2, 256, 4, 256)
    memv = np.stack([R[0]["memv"], R[4]["memv"]], axis=1).reshape(2, 2, 256, 4, 256)
    pool_p = np.stack([R[3]["pool_p"], R[7]["pool_p"]], axis=1)
    conv_p = np.stack([R[3]["conv_p"], R[7]["conv_p"]], axis=1)
    k_s = samp("k_s").reshape(2, 128, 1, 4, 128)
    v_s = samp("v_s").reshape(2, 128, 1, 4, 128)
    pool_s = samp("pool_s").reshape(2, 128, 15, 512)
    conv_s = samp("conv_s").reshape(2, 128, 2, 5632)
    return (y_p, y_s, k_p, v_p, memk, memv, pool_p, conv_p, k_s, v_s, pool_s, conv_s)
```
